# Optimizing a Trainium2 kernel written in Bass

```python
import math
import jax
import jax.numpy as jnp
from jax import lax
import numpy as np

D_MODEL = 1024
BATCH = 8
SEQ = 4096
DEPTH = 2

CTX_LEN = 256
GRID_W = 64

HEAD_DIM = 64
N_MIXERS = 4
GROUP_WIDTH = D_MODEL // N_MIXERS
MIX_WIDTH = N_MIXERS * GROUP_WIDTH
N_MOD = 9
D_FF = 2816

SWA_HEADS = GROUP_WIDTH // HEAD_DIM
SWA_KV_HEADS = 2
SWA_WINDOW = 128
SWA_BLOCK = 128
SSD_HEADS = GROUP_WIDTH // HEAD_DIM
SSD_HEAD_DIM = HEAD_DIM
SSD_INNER = SSD_HEADS * SSD_HEAD_DIM
SSD_STATE = 128
SSD_BC_GROUPS = 2
SSD_CONV = 5
SSD_CHUNK = 128
SSD_CONV_CH = SSD_INNER + 2 * SSD_BC_GROUPS * SSD_STATE
DT_MIN = 0.001
DT_MAX = 0.1
GQA_HEADS = GROUP_WIDTH // HEAD_DIM
GQA_KV_HEADS = 2
GQA_BLOCK = 128
NA_HEADS = GROUP_WIDTH // HEAD_DIM
NA_ROWS = 8
NA_COLS = 16

ROPE_BASE = 10000.0
EPS = 1e-5
ALPHA = (2 * DEPTH) ** 0.25
BETA = (8 * DEPTH) ** -0.25

IN_SPLITS = (SWA_HEADS * HEAD_DIM, SWA_KV_HEADS * HEAD_DIM, SWA_KV_HEADS * HEAD_DIM,
             SSD_INNER, SSD_CONV_CH, 2 * SSD_HEADS,
             GQA_HEADS * HEAD_DIM, GQA_KV_HEADS * HEAD_DIM, GQA_KV_HEADS * HEAD_DIM,
             NA_HEADS * HEAD_DIM, NA_HEADS * HEAD_DIM, NA_HEADS * HEAD_DIM)
IN_WIDTH = sum(IN_SPLITS)

kernel_name = 'hybrid_parallel_group_diffusion_block'


def _heads(t):
    return t.reshape(t.shape[:-1] + (t.shape[-1] // HEAD_DIM, HEAD_DIM))


def _flip(t):
    return jnp.flip(t, axis=1)


def layer_norm(x, g, b):
    xf = x.astype(jnp.float32)
    mu = jnp.mean(xf, -1, keepdims=True)
    var = jnp.mean(jnp.square(xf - mu), -1, keepdims=True)
    return ((xf - mu) * lax.rsqrt(var + EPS) * g.astype(jnp.float32) + b.astype(jnp.float32)).astype(x.dtype)


def rms_norm(x, g):
    xf = x.astype(jnp.float32)
    return (xf * lax.rsqrt(jnp.mean(xf * xf, -1, keepdims=True) + EPS) * g.astype(jnp.float32)).astype(x.dtype)


def modulate(x, shift, scale):
    return x * (1 + scale) + shift


def post_norm(x, f, g, b):
    return layer_norm(ALPHA * x + f, g, b)


def swiglu(h, w_in, w_out):
    a, u = jnp.split(h @ w_in, 2, axis=-1)
    return (jax.nn.silu(a) * u) @ w_out


def rope_tables(S, dtype):
    t = jnp.arange(S)
    pos = jnp.stack([t // GRID_W, t % GRID_W], -1).astype(jnp.float32)
    quarter = HEAD_DIM // 4
    inv = ROPE_BASE ** (-jnp.arange(quarter, dtype=jnp.float32) / quarter)
    ang = pos[:, None, :, None] * inv
    return jnp.cos(ang).astype(dtype), jnp.sin(ang).astype(dtype)


def rope2d(x, cos, sin):
    xr = x.reshape(x.shape[:-1] + (2, 2, HEAD_DIM // 4))
    x1, x2 = xr[..., 0, :], xr[..., 1, :]
    out = jnp.stack([x1 * cos - x2 * sin, x1 * sin + x2 * cos], -2)
    return out.reshape(x.shape)


def dense_attn(q, k, v, sink=None):
    Bq, L, Hq, dh = q.shape
    g = k.shape[2]
    r = Hq // g
    qg = q.reshape(Bq, L, g, r, dh)
    s = jnp.einsum('blgrd,bmgd->bgrlm', qg, k).astype(jnp.float32) * (dh ** -0.5)
    if sink is not None:
        sk = jnp.broadcast_to(sink.astype(jnp.float32).reshape(1, g, r, 1, 1), s.shape[:-1] + (1,))
        p = jax.nn.softmax(jnp.concatenate([s, sk], -1), -1)[..., :-1]
    else:
        p = jax.nn.softmax(s, -1)
    o = jnp.einsum('bgrlm,bmgd->blgrd', p.astype(v.dtype), v)
    return o.reshape(Bq, L, Hq * dh)


def swa_latent(q, k, v, k_ctx, v_ctx, sink):
    Bq, S, Hq, dh = q.shape
    g = k.shape[2]
    r = Hq // g
    Q = SWA_BLOCK
    nb = S // Q
    M = k_ctx.shape[1]
    qb = q.reshape(Bq, nb, Q, g, r, dh)

    def band(t):
        tp = jnp.pad(t, ((0, 0), (Q, Q), (0, 0), (0, 0))).reshape(Bq, nb + 2, Q, g, dh)
        return jnp.concatenate([tp[:, :-2], tp[:, 1:-1], tp[:, 2:]], axis=2)

    kb, vb = band(k), band(v)
    qpos = jnp.arange(nb)[:, None] * Q + jnp.arange(Q)[None]
    kpos = (jnp.arange(nb)[:, None] - 1) * Q + jnp.arange(3 * Q)[None]
    off = kpos[:, None, :] - qpos[:, :, None]
    valid = (jnp.abs(off) <= SWA_WINDOW) & (kpos[:, None, :] >= 0) & (kpos[:, None, :] < S)
    scale = dh ** -0.5
    s_loc = jnp.einsum('bnqgrd,bnkgd->bngrqk', qb, kb).astype(jnp.float32) * scale
    s_loc = jnp.where(valid[None, :, None, None], s_loc, -jnp.inf)
    s_ctx = jnp.einsum('bnqgrd,bmgd->bngrqm', qb, k_ctx).astype(jnp.float32) * scale
    sk = jnp.broadcast_to(sink.astype(jnp.float32).reshape(1, 1, g, r, 1, 1), s_ctx.shape[:-1] + (1,))
    p = jax.nn.softmax(jnp.concatenate([s_loc, s_ctx, sk], -1), -1).astype(v.dtype)
    o = (jnp.einsum('bngrqk,bnkgd->bnqgrd', p[..., :3 * Q], vb)
         + jnp.einsum('bngrqm,bmgd->bnqgrd', p[..., 3 * Q:3 * Q + M], v_ctx))
    return o.reshape(Bq, S, Hq * dh)


def gqa_latent(q, k, v, k_ctx, v_ctx):
    Bq, S, Hq, dh = q.shape
    nb = S // GQA_BLOCK
    k_all = jnp.concatenate([k, k_ctx], axis=1)
    v_all = jnp.concatenate([v, v_ctx], axis=1)
    qb = jnp.moveaxis(q.reshape(Bq, nb, GQA_BLOCK, Hq, dh), 1, 0)
    o = lax.map(lambda qblk: dense_attn(qblk, k_all, v_all), qb)
    return jnp.moveaxis(o, 0, 1).reshape(Bq, S, Hq * dh)


def na_latent(q, k, v, k_ctx, v_ctx, rpb):
    Bq, S, H, dh = q.shape
    W = GRID_W
    rows = S // W
    KH = min(NA_ROWS, rows)
    KW = NA_COLS
    qg = q.reshape(Bq, rows, W, H, dh)
    kg = k.reshape(Bq, rows, W, H, dh)
    vg = v.reshape(Bq, rows, W, H, dh)
    cidx = jnp.arange(W)
    col_idx = jnp.clip(cidx - KW // 2, 0, W - KW)[:, None] + jnp.arange(KW)[None]
    dc = col_idx - cidx[:, None] + NA_COLS - 1
    rpb_c = rpb.astype(jnp.float32)[:, :, dc]
    scale = dh ** -0.5

    def one_row(args):
        r, q_row = args
        rs = jnp.clip(r - KH // 2, 0, rows - KH)
        k_nb = lax.dynamic_slice_in_dim(kg, rs, KH, axis=1)[:, :, col_idx]
        v_nb = lax.dynamic_slice_in_dim(vg, rs, KH, axis=1)[:, :, col_idx]
        dr = rs + jnp.arange(KH) - r + NA_ROWS - 1
        bias = jnp.transpose(jnp.take(rpb_c, dr, axis=1), (0, 2, 1, 3))
        s_nb = jnp.einsum('bwhd,bawkhd->bhwak', q_row, k_nb).astype(jnp.float32) * scale + bias
        s_ctx = jnp.einsum('bwhd,bmhd->bhwm', q_row, k_ctx).astype(jnp.float32) * scale
        p = jax.nn.softmax(jnp.concatenate([s_nb.reshape(Bq, H, W, KH * KW), s_ctx], -1), -1)
        p = p.astype(v.dtype)
        p_nb = p[..., :KH * KW].reshape(Bq, H, W, KH, KW)
        return (jnp.einsum('bhwak,bawkhd->bwhd', p_nb, v_nb)
                + jnp.einsum('bhwm,bmhd->bwhd', p[..., KH * KW:], v_ctx))

    o = lax.map(one_row, (jnp.arange(rows), jnp.moveaxis(qg, 1, 0)))
    return jnp.moveaxis(o, 0, 1).reshape(Bq, S, H * dh)


def dwconv(u, w, b):
    out = lax.conv_general_dilated(u, w.astype(u.dtype), window_strides=(1,), padding='SAME',
                                   dimension_numbers=('NWC', 'WIO', 'NWC'),
                                   feature_group_count=u.shape[-1])
    return jax.nn.silu(out + b)


def ssd_prep(xbc, dt_raw, conv_w, conv_b, dt_bias):
    u = dwconv(xbc, conv_w, conv_b)
    xs, Bm, Cm = jnp.split(u, [SSD_INNER, SSD_INNER + SSD_BC_GROUPS * SSD_STATE], axis=-1)
    lead = u.shape[:-1]
    rep = SSD_HEADS // SSD_BC_GROUPS
    xs = xs.reshape(lead + (SSD_HEADS, SSD_HEAD_DIM))
    Bh = jnp.repeat(Bm.reshape(lead + (SSD_BC_GROUPS, SSD_STATE)), rep, axis=-2)
    Ch = jnp.repeat(Cm.reshape(lead + (SSD_BC_GROUPS, SSD_STATE)), rep, axis=-2)
    dt = jax.nn.softplus(dt_raw.astype(jnp.float32).reshape(lead + (2, SSD_HEADS))
                         + dt_bias.astype(jnp.float32))
    return xs, Bh, Ch, dt


def ssd_chunked(x, dt, A, Bh, Ch, h0):
    Bsz, L, H, P = x.shape
    N = Bh.shape[-1]
    Q = SSD_CHUNK
    nc = L // Q
    xc = x.reshape(Bsz, nc, Q, H, P)
    Bc = Bh.reshape(Bsz, nc, Q, H, N)
    Cc = Ch.reshape(Bsz, nc, Q, H, N)
    dtc = dt.reshape(Bsz, nc, Q, H)
    cum = jnp.cumsum(dtc * A, axis=2)
    seg = cum[:, :, :, None, :] - cum[:, :, None, :, :]
    tril = jnp.tril(jnp.ones((Q, Q), bool))[None, None, :, :, None]
    lmat = jnp.exp(jnp.where(tril, seg, -jnp.inf))
    att = jnp.einsum('bcihn,bcjhn->bcijh', Cc, Bc) * lmat * dtc[:, :, None]
    y = jnp.einsum('bcijh,bcjhp->bcihp', att, xc)
    w_end = jnp.exp(cum[:, :, -1:] - cum) * dtc
    states = jnp.einsum('bcjhn,bcjh,bcjhp->bchpn', Bc, w_end, xc)
    chunk_decay = jnp.exp(cum[:, :, -1])

    def step(h, inp):
        st, dec = inp
        return h * dec[:, :, None, None] + st, h

    h_last, h_start = lax.scan(step, h0, (jnp.moveaxis(states, 1, 0), jnp.moveaxis(chunk_decay, 1, 0)))
    y = y + jnp.einsum('bcihn,bchpn,bcih->bcihp', Cc, jnp.moveaxis(h_start, 0, 1), jnp.exp(cum))
    return y.reshape(Bsz, L, H, P), h_last


def ssd_final_state(x, dt, A, Bh):
    cum = jnp.cumsum(dt * A, axis=1)
    w_end = jnp.exp(cum[:, -1:] - cum) * dt
    return jnp.einsum('blhn,blh,blhp->bhpn', Bh, w_end, x)


def ssd_bidir(xs, dt, A, Bh, Ch, h0f, h0b):
    yf, hf = ssd_chunked(xs, dt[..., 0, :], A[0], Bh, Ch, h0f)
    yb, hb = ssd_chunked(_flip(xs), _flip(dt[..., 1, :]), A[1], _flip(Bh), _flip(Ch), h0b)
    return yf + _flip(yb), hf, hb


def ssd_out(y, xs, z, d_skip):
    y = y + d_skip.astype(jnp.float32)[:, None] * xs
    return (y.reshape(z.shape) * jax.nn.silu(z)).astype(z.dtype)


def merge_groups(outs, g, w_out):
    y = jnp.concatenate(outs, -1)
    yf = y.astype(jnp.float32).reshape(y.shape[:-1] + (N_MIXERS, GROUP_WIDTH))
    yf = yf * lax.rsqrt(jnp.mean(yf * yf, -1, keepdims=True) + EPS)
    yf = yf * g.astype(jnp.float32).reshape(N_MIXERS, GROUP_WIDTH)
    return yf.reshape(y.shape).astype(y.dtype) @ w_out


def project(h, w_in):
    idx = np.cumsum(IN_SPLITS)[:-1].tolist()
    return jnp.split(h @ w_in, idx, axis=-1)


def hybrid_mixer(h, hc, cos, sin, w_in, w_out, norm_g, sink, conv_w, conv_b, dt_bias, a_log,
                 d_skip, q_norm, k_norm, rpb, last):
    aq, ak, av, bz, bxbc, bdt, cq, ck, cv, dq, dk, dv = project(h, w_in)
    aq_c, ak_c, av_c, bz_c, bxbc_c, bdt_c, cq_c, ck_c, cv_c, dq_c, dk_c, dv_c = project(hc, w_in)
    A = -jnp.exp(a_log.astype(jnp.float32))
    ka_c, va_c = _heads(ak_c), _heads(av_c)
    kc_c, vc_c = rms_norm(_heads(ck_c), k_norm), _heads(cv_c)
    kd_c, vd_c = _heads(dk_c), _heads(dv_c)
    xs_c, bh_c, ch_c, dt_c = ssd_prep(bxbc_c, bdt_c, conv_w, conv_b, dt_bias)
    if last:
        hf_c = ssd_final_state(xs_c, dt_c[..., 0, :], A[0], bh_c)
        hb_c = ssd_final_state(_flip(xs_c), _flip(dt_c[..., 1, :]), A[1], _flip(bh_c))
    else:
        zeros = jnp.zeros((xs_c.shape[0], SSD_HEADS, SSD_HEAD_DIM, SSD_STATE), jnp.float32)
        y_c, hf_c, hb_c = ssd_bidir(xs_c, dt_c, A, bh_c, ch_c, zeros, zeros)
    oa = swa_latent(rope2d(_heads(aq), cos, sin), rope2d(_heads(ak), cos, sin), _heads(av), ka_c, va_c, sink)
    xs, bh, ch, dt = ssd_prep(bxbc, bdt, conv_w, conv_b, dt_bias)
    y, _, _ = ssd_bidir(xs, dt, A, bh, ch, hf_c, hb_c)
    ob = ssd_out(y, xs, bz, d_skip)
    oc = gqa_latent(rope2d(rms_norm(_heads(cq), q_norm), cos, sin),
                    rope2d(rms_norm(_heads(ck), k_norm), cos, sin), _heads(cv), kc_c, vc_c)
    od = na_latent(_heads(dq), _heads(dk), _heads(dv), kd_c, vd_c, rpb)
    mix_x = merge_groups([oa, ob, oc, od], norm_g, w_out)
    if last:
        return mix_x, None
    oa_c = dense_attn(_heads(aq_c), ka_c, va_c, sink)
    ob_c = ssd_out(y_c, xs_c, bz_c, d_skip)
    oc_c = dense_attn(rms_norm(_heads(cq_c), q_norm), kc_c, vc_c)
    od_c = dense_attn(_heads(dq_c), kd_c, vd_c)
    return mix_x, merge_groups([oa_c, ob_c, oc_c, od_c], norm_g, w_out)


def setup_inputs(seed: int = 0) -> dict:
    key = jax.random.key(seed)
    ks = jax.random.split(key, 24)
    f32 = jnp.float32
    D = D_MODEL
    L = DEPTH

    def nrm(k, shape, std):
        return jax.random.normal(k, shape, f32) * std

    dt0 = jnp.exp(jax.random.uniform(ks[16], (L, 2, SSD_HEADS), f32)
                  * (math.log(DT_MAX) - math.log(DT_MIN)) + math.log(DT_MIN))
    return {
        'x': nrm(ks[0], (BATCH, SEQ, D), 1.0),
        'c': nrm(ks[1], (BATCH, D), 1.0),
        'ctx': nrm(ks[2], (BATCH, CTX_LEN, D), 1.0),
        'c_ctx': nrm(ks[3], (D,), 1.0),
        'ada_w': nrm(ks[4], (L, D, N_MOD * D), 0.5 * D ** -0.5),
        'ada_b': nrm(ks[5], (L, N_MOD * D), 0.01),
        'ln_g': 1.0 + nrm(ks[6], (L, 3, D), 0.05),
        'ln_b': nrm(ks[7], (L, 3, D), 0.01),
        'ffn1_w_in': nrm(ks[8], (L, D, 2 * D_FF), D ** -0.5),
        'ffn1_w_out': nrm(ks[9], (L, D_FF, D), BETA * D_FF ** -0.5),
        'mix_w_in': nrm(ks[10], (L, D, IN_WIDTH), D ** -0.5),
        'mix_w_out': nrm(ks[11], (L, MIX_WIDTH, D), BETA * MIX_WIDTH ** -0.5),
        'mix_norm_g': 1.0 + nrm(ks[12], (L, MIX_WIDTH), 0.05),
        'swa_sink': nrm(ks[13], (L, SWA_HEADS), 1.0),
        'ssd_conv_w': nrm(ks[14], (L, SSD_CONV, 1, SSD_CONV_CH), SSD_CONV ** -0.5),
        'ssd_conv_b': nrm(ks[15], (L, SSD_CONV_CH), 0.01),
        'ssd_dt_bias': dt0 + jnp.log(-jnp.expm1(-dt0)),
        'ssd_A_log': jnp.log(jax.random.uniform(ks[17], (L, 2, SSD_HEADS), f32, 1.0, 16.0)),
        'ssd_D': 1.0 + nrm(ks[18], (L, SSD_HEADS), 0.1),
        'gqa_q_norm': 1.0 + nrm(ks[19], (L, HEAD_DIM), 0.05),
        'gqa_k_norm': 1.0 + nrm(ks[20], (L, HEAD_DIM), 0.05),
        'na_rpb': nrm(ks[21], (L, NA_HEADS, 2 * NA_ROWS - 1, 2 * NA_COLS - 1), 0.2),
        'ffn2_w_in': nrm(ks[22], (L, D, 2 * D_FF), D ** -0.5),
        'ffn2_w_out': nrm(ks[23], (L, D_FF, D), BETA * D_FF ** -0.5),
    }


def reference(x, c, ctx, c_ctx, ada_w, ada_b, ln_g, ln_b, ffn1_w_in, ffn1_w_out, mix_w_in, mix_w_out,
              mix_norm_g, swa_sink, ssd_conv_w, ssd_conv_b, ssd_dt_bias, ssd_A_log, ssd_D,
              gqa_q_norm, gqa_k_norm, na_rpb, ffn2_w_in, ffn2_w_out):
    Bsz, S, _ = x.shape
    cos, sin = rope_tables(S, x.dtype)
    for l in range(DEPTH):
        last = l == DEPTH - 1
        m = (jax.nn.silu(c) @ ada_w[l] + ada_b[l]).reshape(Bsz, N_MOD, 1, D_MODEL)
        mc = (jax.nn.silu(c_ctx) @ ada_w[l] + ada_b[l]).reshape(N_MOD, 1, D_MODEL)
        x = post_norm(x, 0.5 * m[:, 2] * swiglu(modulate(x, m[:, 0], m[:, 1]), ffn1_w_in[l], ffn1_w_out[l]),
                      ln_g[l, 0], ln_b[l, 0])
        ctx = post_norm(ctx, 0.5 * mc[2] * swiglu(modulate(ctx, mc[0], mc[1]), ffn1_w_in[l], ffn1_w_out[l]),
                        ln_g[l, 0], ln_b[l, 0])
        mix_x, mix_c = hybrid_mixer(modulate(x, m[:, 3], m[:, 4]), modulate(ctx, mc[3], mc[4]), cos, sin,
                                    mix_w_in[l], mix_w_out[l], mix_norm_g[l], swa_sink[l],
                                    ssd_conv_w[l], ssd_conv_b[l], ssd_dt_bias[l], ssd_A_log[l], ssd_D[l],
                                    gqa_q_norm[l], gqa_k_norm[l], na_rpb[l], last)
        x = post_norm(x, m[:, 5] * mix_x, ln_g[l, 1], ln_b[l, 1])
        x = post_norm(x, 0.5 * m[:, 8] * swiglu(modulate(x, m[:, 6], m[:, 7]), ffn2_w_in[l], ffn2_w_out[l]),
                      ln_g[l, 2], ln_b[l, 2])
        if not last:
            ctx = post_norm(ctx, mc[5] * mix_c, ln_g[l, 1], ln_b[l, 1])
            ctx = post_norm(ctx, 0.5 * mc[8] * swiglu(modulate(ctx, mc[6], mc[7]), ffn2_w_in[l], ffn2_w_out[l]),
                            ln_g[l, 2], ln_b[l, 2])
    return x
```

```python
import contextlib
import numpy as np
import concourse.bass as bass
import concourse.mybir as mybir
from concourse.bass_utils import run_bass_kernel_spmd

F32 = mybir.dt.float32
BF16 = mybir.dt.bfloat16
AF = mybir.ActivationFunctionType
ALU = mybir.AluOpType
AX = mybir.AxisListType

D = 1024
S = 4096
NCTX = 256
T = S + NCTX
DEPTH = 2
DFF = 2816
NJ = DFF // 128
EPS = 1e-5
ALPHA = (2 * DEPTH) ** 0.25
NCORES = 8


class Res:
    __slots__ = ("w", "r", "name")

    def __init__(self, name=""):
        self.w = None
        self.r = {}
        self.name = name


class Buf:
    def __init__(self, t, name=""):
        self.t = t
        self.res = Res(name)

    def __getitem__(self, idx):
        return self.t[idx]


def _res(x):
    return x.res if isinstance(x, Buf) else x


class Tracker:
    ENG = ("pe", "act", "dve", "pool", "sp")
    NDS = 24

    def __init__(self, nc, stack):
        self.nc = nc
        self.E = {"pe": nc.tensor, "act": nc.scalar, "dve": nc.vector, "pool": nc.gpsimd, "sp": nc.sync}
        self.dkeys = {q: ["d_%s_%d" % (q, i) for i in range(self.NDS)] for q in ("sp", "pool")}
        keys = list(self.ENG) + self.dkeys["sp"] + self.dkeys["pool"]
        self.keys = keys
        self.sem = {k: stack.enter_context(nc.semaphore("s_" + k)) for k in keys}
        self.cnt = {k: 0 for k in keys}
        self.seen = {e: {k: 0 for k in keys} for e in self.ENG}
        self.pending = {e: 0 for e in self.ENG}
        self.dn = {"sp": 0, "pool": 0}

    def _needs(self, e, reads, writes):
        n = {}

        def add(k, c):
            if c > n.get(k, 0):
                n[k] = c

        for r in reads:
            r = _res(r)
            if r.w is not None:
                add(*r.w)
        for w in writes:
            w = _res(w)
            if w.w is not None and w.w[0] != e:
                add(*w.w)
            for k, c in w.r.items():
                if k != e:
                    add(k, c)
        if e == "pe":
            n.pop("pe", None)
        return n

    def _wait(self, e, needs):
        for k, c in needs.items():
            if c > self.seen[e][k]:
                self.E[e].wait_ge(self.sem[k], c)
                self.seen[e][k] = c

    def op(self, e, fn, reads=(), writes=(), inc=True):
        self._wait(e, self._needs(e, reads, writes))
        ins = fn(self.E[e])
        if inc:
            self.cnt[e] += 1
            ins.then_inc(self.sem[e], 1)
            c = self.cnt[e]
            self.pending[e] = 0
        else:
            c = self.cnt[e] + 1
            self.pending[e] += 1
        wset = set(id(_res(w)) for w in writes)
        for w in writes:
            w = _res(w)
            w.w = (e, c)
            w.r = {}
        for r in reads:
            r = _res(r)
            if id(r) not in wset:
                if c > r.r.get(e, 0):
                    r.r[e] = c
        return ins

    def dma(self, e, out, in_, reads=(), writes=(), **kw):
        k = self.dkeys[e][self.dn[e] % self.NDS]
        self.dn[e] += 1
        needs = self._needs(None, reads, writes)
        if self.cnt[k] > needs.get(k, 0):
            needs[k] = self.cnt[k]
        self._wait(e, needs)
        ins = self.E[e].dma_start(out=out, in_=in_, **kw)
        self.cnt[k] += 16
        ins.then_inc(self.sem[k], 16)
        c = self.cnt[k]
        for w in writes:
            w = _res(w)
            w.w = (k, c)
            w.r = {}
        for r in reads:
            r = _res(r)
            if c > r.r.get(k, 0):
                r.r[k] = c
        return ins

    def barrier(self):
        for e in self.ENG:
            assert self.pending[e] == 0, e
        self._wait("sp", dict(self.cnt))
        self.cnt["sp"] += 1
        self.E["sp"].nop().then_inc(self.sem["sp"], 1)
        tok = {"sp": self.cnt["sp"]}
        for e in self.ENG:
            if e != "sp":
                self._wait(e, tok)
                for k in self.keys:
                    self.seen[e][k] = max(self.seen[e][k], self.cnt[k]) if k != "sp" else self.seen[e][k]


class Rot:
    def __init__(self, bufs):
        self.bufs = bufs
        self.i = 0

    def next(self):
        b = self.bufs[self.i % len(self.bufs)]
        self.i += 1
        return b


class Ctx:
    N = [0]

    def __init__(self, nc, stack):
        self.nc = nc
        self.stack = stack

    def sb(self, shape, dt, name=None):
        Ctx.N[0] += 1
        name = (name or "sb") + "_%d" % Ctx.N[0]
        return Buf(self.stack.enter_context(self.nc.sbuf_tensor(name, list(shape), dt)), name)

    def ps(self, shape, dt, name=None):
        Ctx.N[0] += 1
        name = (name or "ps") + "_%d" % Ctx.N[0]
        return Buf(self.stack.enter_context(self.nc.psum_tensor(name, list(shape), dt)), name)

    def dram(self, name, shape, dt, kind="Internal"):
        return self.nc.dram_tensor(name, list(shape), dt, kind=kind)


def ffn_in_perm():
    idx = []
    for b in range(NJ // 2):
        for jj in range(2):
            j = 2 * b + jj
            idx.extend(range(j * 128, (j + 1) * 128))
        for jj in range(2):
            j = 2 * b + jj
            idx.extend(range(DFF + j * 128, DFF + (j + 1) * 128))
    return np.asarray(idx, dtype=np.int64)


def build_program(cfg):
    nc = bass.Bass("TRN2", target_bir_lowering=False)
    stack = contextlib.ExitStack()
    with stack:
        _build(nc, stack, cfg)
    return nc


def _build(nc, stack, cfg):
    K = Tracker(nc, stack)
    C = Ctx(nc, stack)
    stop_after = cfg.get("stop_after", "all")

    x_in = nc.dram_tensor("x", [S, D], F32, kind="ExternalInput").ap()
    ctx_in = nc.dram_tensor("ctx", [NCTX, D], F32, kind="ExternalInput").ap()
    c2t = nc.dram_tensor("c2t", [D, 2], F32, kind="ExternalInput").ap()
    ada_w = nc.dram_tensor("ada_w", [DEPTH, D, 9 * D], F32, kind="ExternalInput").ap()
    ada_b = nc.dram_tensor("ada_b", [DEPTH, 9 * D], F32, kind="ExternalInput").ap()
    ln_g = nc.dram_tensor("ln_g", [DEPTH, 3, D], F32, kind="ExternalInput").ap()
    ln_b = nc.dram_tensor("ln_b", [DEPTH, 3, D], F32, kind="ExternalInput").ap()
    w1in = nc.dram_tensor("ffn1_w_in", [DEPTH, D, 2 * DFF], F32, kind="ExternalInput").ap()
    w1out = nc.dram_tensor("ffn1_w_out", [DEPTH, DFF, D], F32, kind="ExternalInput").ap()
    w2in = nc.dram_tensor("ffn2_w_in", [DEPTH, D, 2 * DFF], F32, kind="ExternalInput").ap()
    w2out = nc.dram_tensor("ffn2_w_out", [DEPTH, DFF, D], F32, kind="ExternalInput").ap()
    out = nc.dram_tensor("out", [S, D], F32, kind="ExternalOutput").ap()

    modv = nc.dram_tensor("modv", [DEPTH, 2, 9 * D], F32, kind="Internal").ap()
    xs_a = nc.dram_tensor("xs_a", [T, D], F32, kind="Internal").ap()
    xs_b = nc.dram_tensor("xs_b", [T, D], F32, kind="Internal").ap()
    w1in_h = nc.dram_tensor("w1in_h", [DEPTH, D, 2 * DFF], BF16, kind="Internal").ap()
    w1out_h = nc.dram_tensor("w1out_h", [DEPTH, DFF, D], BF16, kind="Internal").ap()
    w2in_h = nc.dram_tensor("w2in_h", [DEPTH, D, 2 * DFF], BF16, kind="Internal").ap()
    w2out_h = nc.dram_tensor("w2out_h", [DEPTH, DFF, D], BF16, kind="Internal").ap()

    NMIX = 3592
    wmix = nc.dram_tensor("wmix", [DEPTH, D, NMIX], F32, kind="ExternalInput").ap()
    wmo = nc.dram_tensor("mix_w_out", [DEPTH, D, D], F32, kind="ExternalInput").ap()
    mixg = nc.dram_tensor("mix_norm_g", [DEPTH, D], F32, kind="ExternalInput").ap()
    cos_t = nc.dram_tensor("cos_t", [128, T], F32, kind="ExternalInput").ap()
    sin_t = nc.dram_tensor("sin_t", [128, T], F32, kind="ExternalInput").ap()
    qk_col = nc.dram_tensor("qk_col", [DEPTH, 128, 4], F32, kind="ExternalInput").ap()
    sink_in = nc.dram_tensor("swa_sink", [DEPTH, 4], F32, kind="ExternalInput").ap()
    convw = nc.dram_tensor("convw", [DEPTH, 128, 6, 5], F32, kind="ExternalInput").ap()
    convb = nc.dram_tensor("convb", [DEPTH, 128, 6], F32, kind="ExternalInput").ap()
    dtb_in = nc.dram_tensor("dt_bias", [DEPTH, 8, 1], F32, kind="ExternalInput").ap()
    alog_in = nc.dram_tensor("a_log", [DEPTH, 8, 1], F32, kind="ExternalInput").ap()
    ssdd_in = nc.dram_tensor("ssd_D", [DEPTH, 4], F32, kind="ExternalInput").ap()
    natab = nc.dram_tensor("natab", [DEPTH, 3, 128, 8 * 4 * 512], F32, kind="ExternalInput").ap()
    maska_in = nc.dram_tensor("maska", [128, 6 * 512], F32, kind="ExternalInput").ap()
    tri_in = nc.dram_tensor("tri_in", [128, 4 * 128], F32, kind="ExternalInput").ap()

    wmix_h = nc.dram_tensor("wmix_h", [DEPTH, D, NMIX], BF16, kind="Internal").ap()
    wmo_h = nc.dram_tensor("wmo_h", [DEPTH, D, D], BF16, kind="Internal").ap()
    QA = nc.dram_tensor("QA", [2, 128, T], BF16, kind="Internal").ap()
    KA = nc.dram_tensor("KA", [1, 128, T], BF16, kind="Internal").ap()
    QC = nc.dram_tensor("QC", [2, 128, T], BF16, kind="Internal").ap()
    KC = nc.dram_tensor("KC", [1, 128, T], BF16, kind="Internal").ap()
    QD = nc.dram_tensor("QD", [2, 128, T], BF16, kind="Internal").ap()
    KD = nc.dram_tensor("KD", [2, 128, T], BF16, kind="Internal").ap()
    XBCT = nc.dram_tensor("XBCT", [6, 128, T], F32, kind="Internal").ap()
    DTT = nc.dram_tensor("DTT", [8, T], F32, kind="Internal").ap()
    VT = nc.dram_tensor("VT", [T, 512], BF16, kind="Internal").ap()
    ZT = nc.dram_tensor("ZT", [T, 256], F32, kind="Internal").ap()
    YM = nc.dram_tensor("YM", [T, D], F32, kind="Internal").ap()

    ident_b = C.sb([128, 128], BF16, "identb")
    ident_f = C.sb([128, 128], F32, "identf")
    K.op("pool", lambda e: e.memset(ident_f[:, :], 0.0), writes=[ident_f])
    K.op("pool", lambda e: e.affine_select(out=ident_f[:, :], in_=ident_f[:, :], pattern=[[-1, 128]], base=0,
                                            channel_multiplier=1, compare_op=ALU.not_equal, fill=1.0),
         reads=[ident_f], writes=[ident_f])
    K.op("dve", lambda e: e.tensor_copy(out=ident_b[:, :], in_=ident_f[:, :]), reads=[ident_f], writes=[ident_b])

    epsc = C.sb([128, 1], F32, "epsc")
    onec = C.sb([128, 1], F32, "onec")
    K.op("pool", lambda e: e.memset(epsc[:, :], EPS), writes=[epsc])
    K.op("pool", lambda e: e.memset(onec[:, :], 1.0), writes=[onec])

    def convert(pairs):
        with contextlib.ExitStack() as st:
            C2 = Ctx(nc, st)
            CH = 4096
            stg = Rot([C2.sb([128, CH], F32, "cvs") for _ in range(3)])
            cvo = Rot([C2.sb([128, CH], BF16, "cvo") for _ in range(3)])
            engs = ["act", "dve", "pool"]
            i = 0
            for src, dst in pairs:
                rows, cols = src.shape
                a = rows // 128
                sv = src.rearrange("(p a) c -> p (a c)", p=128)
                dv = dst.rearrange("(p a) c -> p (a c)", p=128)
                tot = a * cols
                for o in range(0, tot, CH):
                    n = min(CH, tot - o)
                    s_, d_ = stg.next(), cvo.next()
                    K.dma("sp", s_[:, :n], sv[:, o:o + n], writes=[s_])
                    en = engs[i % 3]
                    if en == "act":
                        K.op("act", lambda e, s_=s_, d_=d_, n=n: e.copy(out=d_[:, :n], in_=s_[:, :n]), reads=[s_], writes=[d_])
                    else:
                        K.op(en, lambda e, s_=s_, d_=d_, n=n: e.tensor_copy(out=d_[:, :n], in_=s_[:, :n]), reads=[s_], writes=[d_])
                    K.dma("sp", dv[:, o:o + n], d_[:, :n], reads=[d_])
                    i += 1
        K.barrier()

    def modulation():
        with contextlib.ExitStack() as st:
            C2 = Ctx(nc, st)
            ct = C2.sb([128, 8, 2], F32, "ct")
            K.dma("sp", ct[:, :, :], c2t.rearrange("(p j) m -> p j m", p=128), writes=[ct])
            K.op("act", lambda e: e.activation(out=ct[:, :, :], in_=ct[:, :, :], func=AF.Silu), reads=[ct], writes=[ct])
            wb = Rot([C2.sb([128, 8, 512], F32, "adaw") for _ in range(3)])
            msb = C2.sb([2, 9 * D], F32, "msb")
            bsb = C2.sb([2, 9 * D], F32, "bsb")
            pm = Rot([C2.ps([128, 512], F32, "pm") for _ in range(2)])
            for l in range(DEPTH):
                K.dma("pool", bsb[:, :], ada_b[l:l + 1, :].partition_broadcast(2), writes=[bsb])
                wv = ada_w[l].rearrange("(p j) n -> p j n", p=128)
                for nb in range(18):
                    w_ = wb.next()
                    K.dma("sp", w_[:, :, :], wv[:, :, nb * 512:(nb + 1) * 512], writes=[w_])
                    p_ = pm.next()
                    for j in range(8):
                        K.op("pe", lambda e, j=j, w_=w_, p_=p_: e.matmul(p_[0:2, :], lhsT=ct[:, j, :], rhs=w_[:, j, :],
                                                                      start=(j == 0), stop=(j == 7)),
                             reads=[ct, w_], writes=[p_], inc=(j == 7))
                    K.op("dve", lambda e, p_=p_, nb=nb: e.tensor_tensor(out=msb[:, nb * 512:(nb + 1) * 512], in0=p_[0:2, :],
                                                                     in1=bsb[:, nb * 512:(nb + 1) * 512], op=ALU.add),
                         reads=[p_, bsb], writes=[msb])
                K.dma("sp", modv[l], msb[:, :], reads=[msb])
        K.barrier()

    def ffn_phase(l, sub, w_in_h, w_out_h, segs):
        with contextlib.ExitStack() as st:
            C2 = Ctx(nc, st)
            w2 = C2.sb([128, NJ, D], BF16, "w2")
            K.dma("pool", w2[:, :, :], w_out_h.rearrange("(j p) n -> p j n", p=128), writes=[w2])
            lng = C2.sb([128, D], F32, "lng")
            lnb = C2.sb([128, D], F32, "lnb")
            K.dma("pool", lng[:, :], ln_g[l, sub:sub + 1, :].partition_broadcast(128), writes=[lng])
            K.dma("pool", lnb[:, :], ln_b[l, sub:sub + 1, :].partition_broadcast(128), writes=[lnb])
            shift = C2.sb([128, D], F32, "shift")
            sc1p = C2.sb([128, D], F32, "sc1p")
            gateh = C2.sb([128, D], F32, "gateh")
            xin = Rot([C2.sb([128, D], F32, "xin") for _ in range(2)])
            xr = Rot([C2.sb([128, D], F32, "xr") for _ in range(2)])
            hb = Rot([C2.sb([128, D], BF16, "hb") for _ in range(2)])
            hT = Rot([C2.sb([128, 8, 512], BF16, "hT") for _ in range(2)])
            gT = C2.sb([128, NJ, 512], BF16, "gT")
            w1 = Rot([C2.sb([128, 8, 512], BF16, "w1") for _ in range(3)])
            sa = Rot([C2.sb([128, 512], F32, "sa") for _ in range(2)])
            ybuf = Rot([C2.sb([128, D], F32, "ybuf") for _ in range(2)])
            obuf = Rot([C2.sb([128, D], F32, "obuf") for _ in range(2)])
            stats = Rot([C2.sb([128, 2, 6], F32, "stats") for _ in range(2)])
            mv = Rot([C2.sb([128, 2], F32, "mv") for _ in range(2)])
            rstd = Rot([C2.sb([128, 1], F32, "rstd") for _ in range(2)])
            mhalf = C2.sb([128, 1], F32, "mhalf")
            K.op("pool", lambda e: e.memset(mhalf[:, :], -0.5), writes=[mhalf])
            pT = C2.ps([128, D], BF16, "pT")
            psA = Rot([C2.ps([128, 512], F32, "psA") for _ in range(2)])
            psU = Rot([C2.ps([128, 512], F32, "psU") for _ in range(2)])
            pO = C2.ps([128, D], F32, "pO")
            w1v = w_in_h.rearrange("(kc p) f -> p kc f", p=128)
            mbase = 3 * sub
            for (src, dst, ntok, mrow) in segs:
                K.dma("pool", shift[:, :], modv[l, mrow:mrow + 1, (mbase + 0) * D:(mbase + 1) * D].partition_broadcast(128), writes=[shift])
                K.dma("pool", sc1p[:, :], modv[l, mrow:mrow + 1, (mbase + 1) * D:(mbase + 2) * D].partition_broadcast(128), writes=[sc1p])
                K.dma("pool", gateh[:, :], modv[l, mrow:mrow + 1, (mbase + 2) * D:(mbase + 3) * D].partition_broadcast(128), writes=[gateh])
                K.op("pool", lambda e: e.tensor_scalar(out=sc1p[:, :], in0=sc1p[:, :], scalar1=1.0, scalar2=None, op0=ALU.add),
                     reads=[sc1p], writes=[sc1p])
                K.op("pool", lambda e: e.tensor_scalar(out=gateh[:, :], in0=gateh[:, :], scalar1=0.5, scalar2=None, op0=ALU.mult),
                     reads=[gateh], writes=[gateh])
                for g0 in range(0, ntok, 512):
                    G = min(512, ntok - g0)
                    nt = G // 128
                    hT_ = hT.next()
                    for t in range(nt):
                        r0 = g0 + t * 128
                        xi = xin.next()
                        K.dma("sp", xi[:, :], src[r0:r0 + 128, :], writes=[xi])
                        K.op("pool", lambda e, xi=xi: e.tensor_tensor(out=xi[:, :], in0=xi[:, :], in1=sc1p[:, :], op=ALU.mult),
                             reads=[xi, sc1p], writes=[xi])
                        hb_ = hb.next()
                        K.op("dve", lambda e, xi=xi, hb_=hb_: e.tensor_tensor(out=hb_[:, :], in0=xi[:, :], in1=shift[:, :], op=ALU.add),
                             reads=[xi, shift], writes=[hb_])
                        for kc in range(8):
                            K.op("pe", lambda e, kc=kc, hb_=hb_: e.transpose(out=pT[:, kc * 128:(kc + 1) * 128],
                                                                           in_=hb_[:, kc * 128:(kc + 1) * 128], identity=ident_b[:, :]),
                                 reads=[hb_, ident_b], writes=[pT], inc=(kc == 7))
                        K.op("act", lambda e, t=t, hT_=hT_: e.copy(out=hT_[:, :, t * 128:(t + 1) * 128],
                                                                 in_=pT[:, :].rearrange("p (k n) -> p k n", k=8)),
                             reads=[pT], writes=[hT_])
                    for b in range(NJ // 2):
                        w_ = w1.next()
                        K.dma("sp", w_[:, :, :], w1v[:, :, b * 512:(b + 1) * 512], writes=[w_])
                        for jj in range(2):
                            j = 2 * b + jj
                            A_, U_ = psA.next(), psU.next()
                            for kc in range(8):
                                K.op("pe", lambda e, kc=kc, w_=w_, A_=A_, jj=jj: e.matmul(
                                    A_[:, :G], lhsT=w_[:, kc, jj * 128:(jj + 1) * 128], rhs=hT_[:, kc, :G],
                                    start=(kc == 0), stop=(kc == 7)), reads=[w_, hT_], writes=[A_], inc=(kc == 7))
                            for kc in range(8):
                                K.op("pe", lambda e, kc=kc, w_=w_, U_=U_, jj=jj: e.matmul(
                                    U_[:, :G], lhsT=w_[:, kc, 256 + jj * 128:256 + (jj + 1) * 128], rhs=hT_[:, kc, :G],
                                    start=(kc == 0), stop=(kc == 7)), reads=[w_, hT_], writes=[U_], inc=(kc == 7))
                            sa_ = sa.next()
                            K.op("act", lambda e, A_=A_, sa_=sa_: e.activation(out=sa_[:, :G], in_=A_[:, :G], func=AF.Silu),
                                 reads=[A_], writes=[sa_])
                            K.op("dve", lambda e, U_=U_, sa_=sa_, j=j: e.tensor_tensor(out=gT[:, j, :G], in0=sa_[:, :G], in1=U_[:, :G],
                                                                                     op=ALU.mult),
                                 reads=[sa_, U_], writes=[gT])
                    for t in range(nt):
                        r0 = g0 + t * 128
                        for nh in range(2):
                            for j in range(NJ):
                                K.op("pe", lambda e, j=j, nh=nh, t=t: e.matmul(
                                    pO[:, nh * 512:(nh + 1) * 512], lhsT=gT[:, j, t * 128:(t + 1) * 128],
                                    rhs=w2[:, j, nh * 512:(nh + 1) * 512], start=(j == 0), stop=(j == NJ - 1)),
                                    reads=[gT, w2], writes=[pO], inc=(j == NJ - 1 and nh == 1))
                        xr_ = xr.next()
                        K.dma("pool", xr_[:, :], src[r0:r0 + 128, :], writes=[xr_])
                        y_ = ybuf.next()
                        K.op("dve", lambda e, y_=y_: e.tensor_tensor(out=y_[:, :], in0=pO[:, :], in1=gateh[:, :], op=ALU.mult),
                             reads=[pO, gateh], writes=[y_])
                        K.op("dve", lambda e, y_=y_, xr_=xr_: e.scalar_tensor_tensor(out=y_[:, :], in0=xr_[:, :], scalar=ALPHA, in1=y_[:, :],
                                                                                   op0=ALU.mult, op1=ALU.add),
                             reads=[xr_, y_], writes=[y_])
                        ln_epilogue(C2, y_, lng, lnb, stats.next(), mv.next(), rstd.next(), mhalf, obuf, dst[r0:r0 + 128, :])
        K.barrier()

    def ln_epilogue(C2, y_, lng, lnb, st_, mv_, rs_, mhalf, obuf, dst_ap):
        for hh in range(2):
            K.op("dve", lambda e, hh=hh: e.bn_stats(out=st_[:, hh, :], in_=y_[:, hh * 512:(hh + 1) * 512]), reads=[y_], writes=[st_])
        K.op("dve", lambda e: e.bn_aggr(out=mv_[:, :], in_=st_[:, :, :].rearrange("p a b -> p (a b)")), reads=[st_], writes=[mv_])
        K.op("pool", lambda e: e.tensor_scalar(out=rs_[:, :], in0=mv_[:, 1:2], scalar1=EPS, scalar2=None, op0=ALU.add),
             reads=[mv_], writes=[rs_])
        K.op("pool", lambda e: e.tensor_tensor(out=rs_[:, :], in0=rs_[:, :], in1=mhalf[:, :], op=ALU.pow), reads=[rs_, mhalf], writes=[rs_])
        K.op("dve", lambda e: e.tensor_scalar(out=y_[:, :], in0=y_[:, :], scalar1=mv_[:, 0:1], scalar2=rs_[:, 0:1],
                                              op0=ALU.subtract, op1=ALU.mult), reads=[y_, mv_, rs_], writes=[y_])
        K.op("pool", lambda e: e.tensor_tensor(out=y_[:, :], in0=y_[:, :], in1=lng[:, :], op=ALU.mult), reads=[y_, lng], writes=[y_])
        o_ = obuf.next()
        K.op("pool", lambda e: e.tensor_tensor(out=o_[:, :], in0=y_[:, :], in1=lnb[:, :], op=ALU.add), reads=[y_, lnb], writes=[o_])
        K.dma("sp", dst_ap, o_[:, :], reads=[o_])

    def inproj_phase(l, src, segs):
        with contextlib.ExitStack() as st:
            C2 = Ctx(nc, st)
            wm = C2.sb([128, 8, NMIX], BF16, "wm")
            wv = wmix_h[l].rearrange("(kc p) f -> p kc f", p=128)
            for kc in range(8):
                K.dma("pool", wm[:, kc, :], wv[:, kc, :], writes=[wm])
            cosT = C2.sb([128, T], F32, "cosT")
            sinT = C2.sb([128, T], F32, "sinT")
            K.dma("pool", cosT[:, :], cos_t, writes=[cosT])
            K.dma("pool", sinT[:, :], sin_t, writes=[sinT])
            gcol = C2.sb([128, 4], F32, "gcol")
            K.dma("pool", gcol[:, :], qk_col[l], writes=[gcol])
            bones = C2.sb([128, 128], F32, "bones")
            K.op("pool", lambda e: e.memset(bones[:, :], 0.0), writes=[bones])
            K.op("pool", lambda e: e.memset(bones[0:64, 0:64], 1.0), reads=[bones], writes=[bones])
            K.op("pool", lambda e: e.memset(bones[64:128, 64:128], 1.0), reads=[bones], writes=[bones])
            shift = C2.sb([128, D], F32, "shift")
            sc1p = C2.sb([128, D], F32, "sc1p")
            xin = Rot([C2.sb([128, D], F32, "xin") for _ in range(2)])
            hb = Rot([C2.sb([128, D], BF16, "hb") for _ in range(2)])
            hT = Rot([C2.sb([128, 8, 512], BF16, "hT") for _ in range(2)])
            t1 = Rot([C2.sb([128, 512], F32, "t1") for _ in range(2)])
            t2 = Rot([C2.sb([128, 512], F32, "t2") for _ in range(2)])
            sq = Rot([C2.sb([128, 512], F32, "sq") for _ in range(2)])
            rs = Rot([C2.sb([128, 512], F32, "rs") for _ in range(2)])
            ob = Rot([C2.sb([128, 512], BF16, "ob") for _ in range(3)])
            of = Rot([C2.sb([128, 512], F32, "of") for _ in range(3)])
            vtb = Rot([C2.sb([128, 512], BF16, "vtb") for _ in range(2)])
            ztb = Rot([C2.sb([128, 256], F32, "ztb") for _ in range(2)])
            pT = C2.ps([128, D], BF16, "pT")
            psF = Rot([C2.ps([128, 512], F32, "psF") for _ in range(4)])
            pss = C2.ps([128, 512], F32, "pss")
            pTM = C2.ps([128, 1024], F32, "pTM")

            def fm_tile(i, hT_, G):
                M = 128 if i < 22 else 8
                p_ = psF.next()
                for kc in range(8):
                    K.op("pe", lambda e, kc=kc, p_=p_: e.matmul(p_[0:M, :G], lhsT=wm[:, kc, i * 128:i * 128 + M], rhs=hT_[:, kc, :G],
                                                               start=(kc == 0), stop=(kc == 7)),
                         reads=[wm, hT_], writes=[p_], inc=(kc == 7))
                return p_

            for (tok0, ntok, mrow) in segs:
                K.dma("pool", shift[:, :], modv[l, mrow:mrow + 1, 3 * D:4 * D].partition_broadcast(128), writes=[shift])
                K.dma("pool", sc1p[:, :], modv[l, mrow:mrow + 1, 4 * D:5 * D].partition_broadcast(128), writes=[sc1p])
                K.op("pool", lambda e: e.tensor_scalar(out=sc1p[:, :], in0=sc1p[:, :], scalar1=1.0, scalar2=None, op0=ALU.add),
                     reads=[sc1p], writes=[sc1p])
                for g0 in range(0, ntok, 512):
                    G = min(512, ntok - g0)
                    nt = G // 128
                    c0 = tok0 + g0
                    hT_ = hT.next()
                    for t in range(nt):
                        r0 = c0 + t * 128
                        xi = xin.next()
                        K.dma("sp", xi[:, :], src[r0:r0 + 128, :], writes=[xi])
                        K.op("pool", lambda e, xi=xi: e.tensor_tensor(out=xi[:, :], in0=xi[:, :], in1=sc1p[:, :], op=ALU.mult),
                             reads=[xi, sc1p], writes=[xi])
                        hb_ = hb.next()
                        K.op("dve", lambda e, xi=xi, hb_=hb_: e.tensor_tensor(out=hb_[:, :], in0=xi[:, :], in1=shift[:, :], op=ALU.add),
                             reads=[xi, shift], writes=[hb_])
                        for kc in range(8):
                            K.op("pe", lambda e, kc=kc, hb_=hb_: e.transpose(out=pT[:, kc * 128:(kc + 1) * 128],
                                                                           in_=hb_[:, kc * 128:(kc + 1) * 128], identity=ident_b[:, :]),
                                 reads=[hb_, ident_b], writes=[pT], inc=(kc == 7))
                        K.op("act", lambda e, t=t, hT_=hT_: e.copy(out=hT_[:, :, t * 128:(t + 1) * 128],
                                                                 in_=pT[:, :].rearrange("p (k n) -> p k n", k=8)),
                             reads=[pT], writes=[hT_])
                    plan = [(0, 2, QA[0], None), (1, 3, QA[1], None), (4, 5, KA[0], None),
                            (6, 8, QC[0], 0), (7, 9, QC[1], 0), (10, 11, KC[0], 2)]
                    for (io, isw, dst, ncol) in plan:
                        po = fm_tile(io, hT_, G)
                        psw = fm_tile(isw, hT_, G)
                        t1_, t2_, o_ = t1.next(), t2.next(), ob.next()
                        if ncol is None:
                            K.op("dve", lambda e, po=po, t1_=t1_: e.tensor_tensor(out=t1_[:, :G], in0=po[:, :G], in1=cosT[:, c0:c0 + G], op=ALU.mult),
                                 reads=[po, cosT], writes=[t1_])
                            K.op("dve", lambda e, psw=psw, t2_=t2_: e.tensor_tensor(out=t2_[:, :G], in0=psw[:, :G], in1=sinT[:, c0:c0 + G], op=ALU.mult),
                                 reads=[psw, sinT], writes=[t2_])
                            K.op("pool", lambda e, t1_=t1_, t2_=t2_, o_=o_: e.tensor_tensor(out=o_[:, :G], in0=t1_[:, :G], in1=t2_[:, :G], op=ALU.add),
                                 reads=[t1_, t2_], writes=[o_])
                        else:
                            sq_, rs_ = sq.next(), rs.next()
                            K.op("act", lambda e, po=po, sq_=sq_: e.activation(out=sq_[:, :G], in_=po[:, :G], func=AF.Square),
                                 reads=[po], writes=[sq_])
                            K.op("pe", lambda e, sq_=sq_: e.matmul(pss[:, :G], lhsT=bones[:, :], rhs=sq_[:, :G], start=True, stop=True),
                                 reads=[bones, sq_], writes=[pss])
                            K.op("act", lambda e, sq_=sq_: e.activation(out=sq_[:, :G], in_=pss[:, :G], func=AF.Sqrt, bias=epsc[:, 0:1], scale=1.0 / 64),
                                 reads=[pss, epsc], writes=[sq_])
                            K.op("dve", lambda e, sq_=sq_, rs_=rs_: e.reciprocal(out=rs_[:, :G], in_=sq_[:, :G]), reads=[sq_], writes=[rs_])
                            K.op("dve", lambda e, po=po, t1_=t1_: e.scalar_tensor_tensor(out=t1_[:, :G], in0=po[:, :G], scalar=gcol[:, ncol:ncol + 1],
                                                                                       in1=cosT[:, c0:c0 + G], op0=ALU.mult, op1=ALU.mult),
                                 reads=[po, cosT, gcol], writes=[t1_])
                            K.op("dve", lambda e, psw=psw, t2_=t2_: e.scalar_tensor_tensor(out=t2_[:, :G], in0=psw[:, :G], scalar=gcol[:, ncol + 1:ncol + 2],
                                                                                         in1=sinT[:, c0:c0 + G], op0=ALU.mult, op1=ALU.mult),
                                 reads=[psw, sinT, gcol], writes=[t2_])
                            K.op("pool", lambda e, t1_=t1_, t2_=t2_: e.tensor_tensor(out=t1_[:, :G], in0=t1_[:, :G], in1=t2_[:, :G], op=ALU.add),
                                 reads=[t1_, t2_], writes=[t1_])
                            K.op("pool", lambda e, t1_=t1_, rs_=rs_, o_=o_: e.tensor_tensor(out=o_[:, :G], in0=t1_[:, :G], in1=rs_[:, :G], op=ALU.mult),
                                 reads=[t1_, rs_], writes=[o_])
                        K.dma("sp", dst[:, c0:c0 + G], o_[:, :G], reads=[o_])
                    for (i, dst) in [(12, QD[0]), (13, QD[1]), (14, KD[0]), (15, KD[1])]:
                        p_ = fm_tile(i, hT_, G)
                        o_ = ob.next()
                        K.op("act", lambda e, p_=p_, o_=o_: e.copy(out=o_[:, :G], in_=p_[:, :G]), reads=[p_], writes=[o_])
                        K.dma("sp", dst[:, c0:c0 + G], o_[:, :G], reads=[o_])
                    for i in range(16, 23):
                        M = 128 if i < 22 else 8
                        p_ = fm_tile(i, hT_, G)
                        o_ = of.next()
                        K.op("act", lambda e, p_=p_, o_=o_, M=M: e.copy(out=o_[0:M, :G], in_=p_[0:M, :G]), reads=[p_], writes=[o_])
                        dst = XBCT[i - 16] if i < 22 else DTT
                        K.dma("sp", dst[0:M, c0:c0 + G], o_[0:M, :G], reads=[o_])
                    for t in range(nt):
                        r0 = c0 + t * 128
                        for (n0, n1) in [(0, 512), (512, 768)]:
                            for kc in range(8):
                                K.op("pe", lambda e, kc=kc, t=t, n0=n0, n1=n1: e.matmul(
                                    pTM[:, n0:n1], lhsT=hT_[:, kc, t * 128:(t + 1) * 128], rhs=wm[:, kc, 22 * 128 + 8 + n0:22 * 128 + 8 + n1],
                                    start=(kc == 0), stop=(kc == 7)), reads=[wm, hT_], writes=[pTM], inc=(kc == 7 and n0 == 512))
                        v_, z_ = vtb.next(), ztb.next()
                        K.op("act", lambda e, v_=v_: e.copy(out=v_[:, :], in_=pTM[:, 0:512]), reads=[pTM], writes=[v_])
                        K.op("dve", lambda e, z_=z_: e.tensor_copy(out=z_[:, :], in_=pTM[:, 512:768]), reads=[pTM], writes=[z_])
                        K.dma("sp", VT[r0:r0 + 128, :], v_[:, :], reads=[v_])
                        K.dma("sp", ZT[r0:r0 + 128, :], z_[:, :], reads=[z_])
        K.barrier()

    def attention_phase(l, kind, with_ctx_q):
        with contextlib.ExitStack() as st:
            C2 = Ctx(nc, st)
            Qsrc = {"A": QA, "C": QC, "D": QD}[kind]
            Ksrc = {"A": KA, "C": KC, "D": KD}[kind]
            nkt = 2 if kind == "D" else 1
            nkv = 4 if kind == "D" else 2
            vcol0 = {"A": 0, "C": 128, "D": 256}[kind]
            ycol0 = {"A": 0, "C": 512, "D": 768}[kind]
            QT = C2.sb([128, 2, T], BF16, "QT")
            KT = C2.sb([128, nkt, T], BF16, "KT")
            for j in range(2):
                K.dma("sp", QT[:, j, :], Qsrc[j], writes=[QT])
            for j in range(nkt):
                K.dma("sp", KT[:, j, :], Ksrc[j], writes=[KT])
            vraw = C2.sb([128, 34, nkv * 64], BF16, "vraw")
            K.dma("pool", vraw[:, :, :], VT[:, vcol0:vcol0 + nkv * 64].rearrange("(c p) f -> p c f", p=128), writes=[vraw])
            Vp = C2.sb([128, 34, nkv, 65], BF16, "Vp")
            K.op("pool", lambda e: e.memset(Vp[:, :, :, :], 1.0), writes=[Vp])
            for g in range(nkv):
                K.op("dve", lambda e, g=g: e.tensor_copy(out=Vp[:, :, g, 0:64], in_=vraw[:, :, g * 64:(g + 1) * 64]),
                     reads=[vraw, Vp], writes=[Vp])
            esink = None
            if kind == "A":
                esink = C2.sb([128, 4], F32, "esink")
                K.dma("pool", esink[:, :], sink_in[l:l + 1, :].partition_broadcast(128), writes=[esink])
                K.op("act", lambda e: e.activation(out=esink[:, :], in_=esink[:, :], func=AF.Exp), reads=[esink], writes=[esink])
                mf = C2.sb([128, 6 * 512], F32, "mf")
                K.dma("pool", mf[:, :], maska_in, writes=[mf])
                maskb = C2.sb([128, 6, 512], BF16, "maskb")
                K.op("dve", lambda e: e.tensor_copy(out=maskb[:, :, :], in_=mf[:, :].rearrange("p (a b) -> p a b", a=6)), reads=[mf], writes=[maskb])
            if kind == "D":
                tab = C2.sb([128, 8, 4, 512], F32, "natab")
            PT = Rot([C2.sb([128, 512], BF16, "PT") for _ in range(3)])
            sbias = Rot([C2.sb([128, 512], F32, "sbias") for _ in range(2)])
            oT = Rot([C2.sb([128, 512], F32, "oT") for _ in range(2)])
            rden = Rot([C2.sb([128, 4], F32, "rden") for _ in range(2)])
            ytile = Rot([C2.sb([128, 4, 256], F32, "ytile") for _ in range(2)])
            psS = Rot([C2.ps([128, 512], F32, "psS") for _ in range(3)])
            pOut = Rot([C2.ps([128, 512], F32, "pOut") for _ in range(2)])
            pTr = C2.ps([128, 4, 128], F32, "pTr")

            blocks = [(s * 512, 512, s) for s in range(8)]
            if with_ctx_q:
                blocks.append((S, NCTX, None))
            cur_tab = None
            for (q0, G, s) in blocks:
                nt = G // 128
                yt_ = ytile.next()
                if kind == "D" and s is not None:
                    ttype = 0 if s == 0 else (2 if s == 7 else 1)
                    if ttype != cur_tab:
                        K.dma("sp", tab[:, :, :, :].rearrange("p a h q -> p (a h q)"), natab[l, ttype], writes=[tab])
                        cur_tab = ttype
                for h in range(4):
                    if kind == "D":
                        jt, po, kt, g = h // 2, (h % 2) * 64, h // 2, h
                    else:
                        jt, po, kt, g = h % 2, (h // 2) * 64, 0, h // 2
                    if s is None:
                        chunks = [(32, None), (33, None)]
                    elif kind == "C":
                        chunks = [(c, None) for c in range(34)]
                    elif kind == "A":
                        chunks = [(4 * s - 1 + i, i) for i in range(6) if 0 <= 4 * s - 1 + i < 32] + [(32, None), (33, None)]
                    else:
                        chunks = [(4 * s - 2 + i, i) for i in range(8) if 0 <= 4 * s - 2 + i < 32] + [(32, None), (33, None)]
                    po_ = pOut.next()
                    for ci, (c, mi) in enumerate(chunks):
                        ps_ = psS.next()
                        K.op("pe", lambda e, ps_=ps_, c=c: e.matmul(ps_[:, :G], lhsT=KT[po:po + 64, kt, c * 128:(c + 1) * 128],
                                                                  rhs=QT[po:po + 64, jt, q0:q0 + G], start=True, stop=True),
                             reads=[KT, QT], writes=[ps_])
                        pt_ = PT.next()
                        if kind == "D" and mi is not None:
                            sb_ = sbias.next()
                            K.op("dve", lambda e, ps_=ps_, sb_=sb_, mi=mi: e.scalar_tensor_tensor(
                                out=sb_[:, :G], in0=ps_[:, :G], scalar=0.125, in1=tab[:, mi, h, :G], op0=ALU.mult, op1=ALU.add),
                                reads=[ps_, tab], writes=[sb_])
                            K.op("act", lambda e, sb_=sb_, pt_=pt_: e.activation(out=pt_[:, :G], in_=sb_[:, :G], func=AF.Exp),
                                 reads=[sb_], writes=[pt_])
                        else:
                            K.op("act", lambda e, ps_=ps_, pt_=pt_: e.activation(out=pt_[:, :G], in_=ps_[:, :G], func=AF.Exp, scale=0.125),
                                 reads=[ps_], writes=[pt_])
                            if kind == "A" and mi is not None:
                                K.op("pool", lambda e, pt_=pt_, mi=mi: e.tensor_tensor(out=pt_[:, :G], in0=pt_[:, :G], in1=maskb[:, mi, :G], op=ALU.mult),
                                     reads=[pt_, maskb], writes=[pt_])
                        K.op("pe", lambda e, pt_=pt_, c=c, ci=ci: e.matmul(po_[0:65, :G], lhsT=Vp[:, c, g, :], rhs=pt_[:, :G],
                                                                         start=(ci == 0), stop=(ci == len(chunks) - 1)),
                             reads=[Vp, pt_], writes=[po_])
                    o_ = oT.next()
                    K.op("act", lambda e, o_=o_: e.copy(out=o_[0:65, :G], in_=po_[0:65, :G]), reads=[po_], writes=[o_])
                    for qt in range(nt):
                        K.op("pe", lambda e, o_=o_, qt=qt: e.transpose(out=pTr[:, qt, 0:65], in_=o_[0:65, qt * 128:(qt + 1) * 128],
                                                                      identity=ident_f[0:65, 0:65]),
                             reads=[o_, ident_f], writes=[pTr], inc=(qt == nt - 1))
                    rd_ = rden.next()
                    if kind == "A":
                        K.op("dve", lambda e, rd_=rd_: e.tensor_scalar(out=rd_[:, 0:nt], in0=pTr[:, 0:nt, 64], scalar1=esink[:, h:h + 1], scalar2=None,
                                                                      op0=ALU.add), reads=[pTr, esink], writes=[rd_])
                        K.op("dve", lambda e, rd_=rd_: e.reciprocal(out=rd_[:, 0:nt], in_=rd_[:, 0:nt]), reads=[rd_], writes=[rd_])
                    else:
                        K.op("dve", lambda e, rd_=rd_: e.reciprocal(out=rd_[:, 0:nt], in_=pTr[:, 0:nt, 64]), reads=[pTr], writes=[rd_])
                    for qt in range(nt):
                        K.op("dve", lambda e, rd_=rd_, qt=qt: e.tensor_scalar(out=yt_[:, qt, h * 64:(h + 1) * 64], in0=pTr[:, qt, 0:64],
                                                                            scalar1=rd_[:, qt:qt + 1], scalar2=None, op0=ALU.mult),
                             reads=[pTr, rd_, yt_], writes=[yt_])
                for qt in range(nt):
                    K.dma("sp", YM[q0 + qt * 128:q0 + (qt + 1) * 128, ycol0:ycol0 + 256], yt_[:, qt, :], reads=[yt_])
        K.barrier()

    def merge_phase(l, src, dst, segs):
        with contextlib.ExitStack() as st:
            C2 = Ctx(nc, st)
            wo = C2.sb([128, 8, D], BF16, "wo")
            K.dma("pool", wo[:, :, :], wmo_h[l].rearrange("(kc p) n -> p kc n", p=128), writes=[wo])
            lng = C2.sb([128, D], F32, "lng")
            lnb = C2.sb([128, D], F32, "lnb")
            gm = C2.sb([128, D], F32, "gm")
            K.dma("pool", lng[:, :], ln_g[l, 1:2, :].partition_broadcast(128), writes=[lng])
            K.dma("pool", lnb[:, :], ln_b[l, 1:2, :].partition_broadcast(128), writes=[lnb])
            K.dma("pool", gm[:, :], mixg[l:l + 1, :].partition_broadcast(128), writes=[gm])
            gate = C2.sb([128, D], F32, "gate")
            yin = Rot([C2.sb([128, D], F32, "yin") for _ in range(2)])
            junk = C2.sb([128, 256], F32, "junk")
            ss = Rot([C2.sb([128, 4], F32, "ss") for _ in range(2)])
            yb = Rot([C2.sb([128, D], BF16, "yb") for _ in range(2)])
            yT = Rot([C2.sb([128, 8, 128], BF16, "yT") for _ in range(2)])
            xr = Rot([C2.sb([128, D], F32, "xr") for _ in range(2)])
            ybuf = Rot([C2.sb([128, D], F32, "ybuf") for _ in range(2)])
            obuf = Rot([C2.sb([128, D], F32, "obuf") for _ in range(2)])
            stats = Rot([C2.sb([128, 2, 6], F32, "stats") for _ in range(2)])
            mv = Rot([C2.sb([128, 2], F32, "mv") for _ in range(2)])
            rstd = Rot([C2.sb([128, 1], F32, "rstd") for _ in range(2)])
            mhalf = C2.sb([128, 4], F32, "mhalf")
            K.op("pool", lambda e: e.memset(mhalf[:, :], -0.5), writes=[mhalf])
            mhalf1 = C2.sb([128, 1], F32, "mhalf1")
            K.op("pool", lambda e: e.memset(mhalf1[:, :], -0.5), writes=[mhalf1])
            pT = C2.ps([128, D], BF16, "pT")
            pO = Rot([C2.ps([128, D], F32, "pO") for _ in range(2)])
            for (tok0, ntok, mrow) in segs:
                K.dma("pool", gate[:, :], modv[l, mrow:mrow + 1, 5 * D:6 * D].partition_broadcast(128), writes=[gate])
                for r0 in range(tok0, tok0 + ntok, 128):
                    y_ = yin.next()
                    K.dma("sp", y_[:, :], YM[r0:r0 + 128, :], writes=[y_])
                    ss_ = ss.next()
                    for g in range(4):
                        K.op("act", lambda e, g=g, y_=y_, ss_=ss_: e.activation(out=junk[:, :], in_=y_[:, g * 256:(g + 1) * 256], func=AF.Square,
                                                                             accum_out=ss_[:, g:g + 1]), reads=[y_], writes=[junk, ss_])
                    K.op("pool", lambda e, ss_=ss_: e.tensor_scalar(out=ss_[:, :], in0=ss_[:, :], scalar1=1.0 / 256, scalar2=EPS, op0=ALU.mult, op1=ALU.add),
                         reads=[ss_], writes=[ss_])
                    K.op("pool", lambda e, ss_=ss_: e.tensor_tensor(out=ss_[:, :], in0=ss_[:, :], in1=mhalf[:, :], op=ALU.pow), reads=[ss_, mhalf], writes=[ss_])
                    for g in range(4):
                        K.op("dve", lambda e, g=g, y_=y_, ss_=ss_: e.tensor_scalar(out=y_[:, g * 256:(g + 1) * 256], in0=y_[:, g * 256:(g + 1) * 256],
                                                                                scalar1=ss_[:, g:g + 1], scalar2=None, op0=ALU.mult),
                             reads=[y_, ss_], writes=[y_])
                    yb_ = yb.next()
                    K.op("pool", lambda e, y_=y_, yb_=yb_: e.tensor_tensor(out=yb_[:, :], in0=y_[:, :], in1=gm[:, :], op=ALU.mult), reads=[y_, gm], writes=[yb_])
                    for kc in range(8):
                        K.op("pe", lambda e, kc=kc, yb_=yb_: e.transpose(out=pT[:, kc * 128:(kc + 1) * 128], in_=yb_[:, kc * 128:(kc + 1) * 128],
                                                                       identity=ident_b[:, :]), reads=[yb_, ident_b], writes=[pT], inc=(kc == 7))
                    yT_ = yT.next()
                    K.op("act", lambda e, yT_=yT_: e.copy(out=yT_[:, :, :], in_=pT[:, :].rearrange("p (k n) -> p k n", k=8)), reads=[pT], writes=[yT_])
                    pO_ = pO.next()
                    for nh in range(2):
                        for kc in range(8):
                            K.op("pe", lambda e, kc=kc, nh=nh, yT_=yT_, pO_=pO_: e.matmul(pO_[:, nh * 512:(nh + 1) * 512], lhsT=yT_[:, kc, :],
                                                                                       rhs=wo[:, kc, nh * 512:(nh + 1) * 512], start=(kc == 0), stop=(kc == 7)),
                                 reads=[yT_, wo], writes=[pO_], inc=(kc == 7 and nh == 1))
                    xr_ = xr.next()
                    K.dma("pool", xr_[:, :], src[r0:r0 + 128, :], writes=[xr_])
                    o_ = ybuf.next()
                    K.op("dve", lambda e, o_=o_, pO_=pO_: e.tensor_tensor(out=o_[:, :], in0=pO_[:, :], in1=gate[:, :], op=ALU.mult),
                         reads=[pO_, gate], writes=[o_])
                    K.op("dve", lambda e, o_=o_, xr_=xr_: e.scalar_tensor_tensor(out=o_[:, :], in0=xr_[:, :], scalar=ALPHA, in1=o_[:, :],
                                                                               op0=ALU.mult, op1=ALU.add), reads=[xr_, o_], writes=[o_])
                    ln_epilogue(C2, o_, lng, lnb, stats.next(), mv.next(), rstd.next(), mhalf1, obuf, dst[r0:r0 + 128, :])
        K.barrier()

    def ssd_phase(l, last):
        with contextlib.ExitStack() as st:
            C2 = Ctx(nc, st)
            xtok = C2.sb([128, 34, 256], F32, "xtok")
            xtok_bf = C2.sb([128, 34, 256], BF16, "xtokb")
            Btok_bf = C2.sb([128, 34, 256], BF16, "Btokb")
            BT_bf = C2.sb([128, 2, T], BF16, "BTb")
            CT_bf = C2.sb([128, 2, T], BF16, "CTb")
            dtT = C2.sb([128, 34, 8], F32, "dtT")
            aT = C2.sb([128, 34, 8], F32, "aT")
            tri = C2.sb([128, 4, 128], F32, "tri")
            K.dma("pool", tri[:, :, :].rearrange("p a b -> p (a b)"), tri_in, writes=[tri])
            ones_f = C2.sb([128, 128], F32, "onesf")
            K.op("pool", lambda e: e.memset(ones_f[:, :], 1.0), writes=[ones_f])
            dcol = C2.sb([128, 4], F32, "dcol")
            K.dma("pool", dcol[:, :], ssdd_in[l:l + 1, :].partition_broadcast(128), writes=[dcol])
            with contextlib.ExitStack() as st2:
                C3 = Ctx(nc, st2)
                cw = C3.sb([128, 6, 5], F32, "cw")
                cb = C3.sb([128, 6], F32, "cb")
                K.dma("pool", cw[:, :, :], convw[l], writes=[cw])
                K.dma("pool", cb[:, :], convb[l], writes=[cb])
                xi_r = Rot([C3.sb([128, T], F32, "cxi") for _ in range(2)])
                acc_r = Rot([C3.sb([128, T], F32, "cacc") for _ in range(1)])
                uf = Rot([C3.sb([128, T], F32, "cuf") for _ in range(1)])
                ptf = Rot([C3.ps([128, 4, 128], F32, "ptf") for _ in range(2)])
                ptb = Rot([C3.ps([128, 8, 128], BF16, "ptb") for _ in range(2)])
                for i in range(6):
                    xi, acc = xi_r.next(), acc_r.next()
                    K.dma("sp", xi[:, :], XBCT[i], writes=[xi])
                    K.op("dve", lambda e, xi=xi, acc=acc, i=i: e.tensor_scalar(out=acc[:, :], in0=xi[:, :], scalar1=cw[:, i, 2:3], scalar2=cb[:, i:i + 1],
                                                                             op0=ALU.mult, op1=ALU.add), reads=[xi, cw, cb], writes=[acc])
                    for k in (0, 1, 3, 4):
                        s_ = k - 2
                        for (lo, hi) in [(0, S), (S, T)]:
                            olo = max(lo, lo - s_)
                            ohi = min(hi, hi - s_)
                            K.op("dve", lambda e, xi=xi, acc=acc, i=i, k=k, olo=olo, ohi=ohi, s_=s_: e.scalar_tensor_tensor(
                                out=acc[:, olo:ohi], in0=xi[:, olo + s_:ohi + s_], scalar=cw[:, i, k:k + 1], in1=acc[:, olo:ohi],
                                op0=ALU.mult, op1=ALU.add), reads=[xi, acc, cw], writes=[acc])
                    if i < 2:
                        u_ = uf.next()
                        K.op("act", lambda e, acc=acc, u_=u_: e.activation(out=u_[:, :], in_=acc[:, :], func=AF.Silu), reads=[acc], writes=[u_])
                        for c0 in range(0, 34, 4):
                            n = min(4, 34 - c0)
                            p_ = ptf.next()
                            for cc in range(n):
                                c = c0 + cc
                                K.op("pe", lambda e, p_=p_, u_=u_, c=c, cc=cc: e.transpose(out=p_[:, cc, :], in_=u_[:, c * 128:(c + 1) * 128],
                                                                                         identity=ident_f[:, :]),
                                     reads=[u_, ident_f], writes=[p_], inc=(cc == n - 1))
                            K.op("act", lambda e, p_=p_, c0=c0, n=n, i=i: e.copy(out=xtok[:, c0:c0 + n, i * 128:(i + 1) * 128], in_=p_[:, 0:n, :]),
                                 reads=[p_], writes=[xtok])
                    elif i < 4:
                        g = i - 2
                        K.op("act", lambda e, acc=acc, g=g: e.activation(out=BT_bf[:, g, :], in_=acc[:, :], func=AF.Silu), reads=[acc], writes=[BT_bf])
                        for c0 in range(0, 34, 8):
                            n = min(8, 34 - c0)
                            p_ = ptb.next()
                            for cc in range(n):
                                c = c0 + cc
                                K.op("pe", lambda e, p_=p_, g=g, c=c, cc=cc: e.transpose(out=p_[:, cc, :], in_=BT_bf[:, g, c * 128:(c + 1) * 128],
                                                                                       identity=ident_b[:, :]),
                                     reads=[BT_bf, ident_b], writes=[p_], inc=(cc == n - 1))
                            K.op("dve", lambda e, p_=p_, c0=c0, n=n, g=g: e.tensor_copy(out=Btok_bf[:, c0:c0 + n, g * 128:(g + 1) * 128], in_=p_[:, 0:n, :]),
                                 reads=[p_], writes=[Btok_bf])
                    else:
                        g = i - 4
                        K.op("act", lambda e, acc=acc, g=g: e.activation(out=CT_bf[:, g, :], in_=acc[:, :], func=AF.Silu), reads=[acc], writes=[CT_bf])
                K.op("dve", lambda e: e.tensor_copy(out=xtok_bf[:, :, :], in_=xtok[:, :, :]), reads=[xtok], writes=[xtok_bf])
                K.barrier()
            with contextlib.ExitStack() as st2:
                C3 = Ctx(nc, st2)
                pdt = C3.ps([128, 34, 8], F32, "pdt")
                pat = C3.ps([128, 34, 8], F32, "pat")
                dtr = C3.sb([8, T], F32, "dtr")
                ar = C3.sb([8, T], F32, "ar")
                dtb = C3.sb([8, 1], F32, "dtb")
                alog = C3.sb([8, 1], F32, "alog")
                K.dma("sp", dtr[:, :], DTT, writes=[dtr])
                K.dma("sp", dtb[:, :], dtb_in[l], writes=[dtb])
                K.dma("sp", alog[:, :], alog_in[l], writes=[alog])
                K.op("act", lambda e: e.activation(out=dtr[:, :], in_=dtr[:, :], func=AF.Exp, bias=dtb[:, 0:1], scale=1.0), reads=[dtr, dtb], writes=[dtr])
                K.op("act", lambda e: e.activation(out=dtr[:, :], in_=dtr[:, :], func=AF.Ln, bias=onec[0:8, 0:1], scale=1.0), reads=[dtr, onec], writes=[dtr])
                K.op("act", lambda e: e.activation(out=alog[:, :], in_=alog[:, :], func=AF.Exp), reads=[alog], writes=[alog])
                K.op("dve", lambda e: e.tensor_scalar(out=alog[:, :], in0=alog[:, :], scalar1=-1.0, scalar2=None, op0=ALU.mult), reads=[alog], writes=[alog])
                K.op("dve", lambda e: e.tensor_scalar(out=ar[:, :], in0=dtr[:, :], scalar1=alog[:, 0:1], scalar2=None, op0=ALU.mult),
                     reads=[dtr, alog], writes=[ar])
                for c in range(34):
                    K.op("pe", lambda e, c=c: e.transpose(out=pdt[:, c, :], in_=dtr[0:8, c * 128:(c + 1) * 128], identity=ident_f[0:8, 0:8]),
                         reads=[dtr, ident_f], writes=[pdt], inc=(c == 33))
                for c in range(34):
                    K.op("pe", lambda e, c=c: e.transpose(out=pat[:, c, :], in_=ar[0:8, c * 128:(c + 1) * 128], identity=ident_f[0:8, 0:8]),
                         reads=[ar, ident_f], writes=[pat], inc=(c == 33))
                K.op("dve", lambda e: e.tensor_copy(out=dtT[:, :, :], in_=pdt[:, :, :]), reads=[pdt], writes=[dtT])
                K.op("dve", lambda e: e.tensor_copy(out=aT[:, :, :], in_=pat[:, :, :]), reads=[pat], writes=[aT])
                K.barrier()
            with contextlib.ExitStack() as st2:
                C3 = Ctx(nc, st2)
                ybk = C3.sb([128, 34, 256], F32, "ybk")
                hst = C3.sb([128, 4, 64], F32, "hst")
                hst_bf = C3.sb([128, 4, 64], BF16, "hstb")
                ncum = Rot([C3.sb([128, 4], F32, "ncum") for _ in range(2)])
                ecum = Rot([C3.sb([128, 4], F32, "ecum") for _ in range(2)])
                wend = Rot([C3.sb([128, 4], F32, "wend") for _ in range(2)])
                decay = Rot([C3.sb([128, 4], F32, "decay") for _ in range(2)])
                ta = Rot([C3.sb([128, 128], F32, "ta") for _ in range(3)])
                Eb = Rot([C3.sb([128, 128], F32, "Eb") for _ in range(3)])
                attT = Rot([C3.sb([128, 128], BF16, "attT") for _ in range(3)])
                xw = Rot([C3.sb([128, 4, 64], BF16, "xw") for _ in range(2)])
                accb = Rot([C3.sb([128, 256], F32, "accb") for _ in range(2)])
                zb = Rot([C3.sb([128, 256], F32, "zb") for _ in range(2)])
                yo = Rot([C3.sb([128, 256], F32, "yo") for _ in range(2)])
                pcs = Rot([C3.ps([128, 8], F32, "pcs") for _ in range(1)])
                pG = C3.ps([128, 256], F32, "pG")
                pI = C3.ps([128, 256], F32, "pI")
                pY = C3.ps([128, 256], F32, "pY")
                pS = C3.ps([128, 256], F32, "pS")
                pr = Rot([C3.ps([128, 128], F32, "pr") for _ in range(2)])
                for d in (1, 0):
                    K.op("dve", lambda e: e.memset(hst[:, :, :], 0.0), reads=[hst], writes=[hst])
                    K.op("dve", lambda e: e.memset(hst_bf[:, :, :], 0.0), reads=[hst_bf], writes=[hst_bf])
                    order = [33, 32] + list(range(31, -1, -1)) if d == 1 else [32, 33] + list(range(32))
                    triM = tri[:, 0, :] if d == 0 else tri[:, 1, :]
                    negM = tri[:, 2, :] if d == 0 else tri[:, 3, :]
                    for c in order:
                        cs = slice(c * 128, (c + 1) * 128)
                        pc_ = pcs.next()
                        K.op("pe", lambda e, pc_=pc_: e.matmul(pc_[:, 0:4], lhsT=triM, rhs=aT[:, c, 4 * d:4 * d + 4], start=True, stop=True),
                             reads=[tri, aT], writes=[pc_])
                        K.op("pe", lambda e, pc_=pc_: e.matmul(pc_[:, 4:8], lhsT=ones_f[:, :], rhs=aT[:, c, 4 * d:4 * d + 4], start=True, stop=True),
                             reads=[ones_f, aT], writes=[pc_])
                        nc_, ec_, we_, de_ = ncum.next(), ecum.next(), wend.next(), decay.next()
                        K.op("dve", lambda e, pc_=pc_, nc_=nc_: e.tensor_scalar(out=nc_[:, :], in0=pc_[:, 0:4], scalar1=-1.0, scalar2=None, op0=ALU.mult),
                             reads=[pc_], writes=[nc_])
                        K.op("act", lambda e, pc_=pc_, ec_=ec_: e.activation(out=ec_[:, :], in_=pc_[:, 0:4], func=AF.Exp), reads=[pc_], writes=[ec_])
                        K.op("dve", lambda e, pc_=pc_, nc_=nc_, we_=we_: e.tensor_tensor(out=we_[:, :], in0=pc_[:, 4:8], in1=nc_[:, :], op=ALU.add),
                             reads=[pc_, nc_], writes=[we_])
                        K.op("act", lambda e, we_=we_: e.activation(out=we_[:, :], in_=we_[:, :], func=AF.Exp), reads=[we_], writes=[we_])
                        K.op("dve", lambda e, we_=we_: e.tensor_tensor(out=we_[:, :], in0=we_[:, :], in1=dtT[:, c, 4 * d:4 * d + 4], op=ALU.mult),
                             reads=[we_, dtT], writes=[we_])
                        K.op("act", lambda e, pc_=pc_, de_=de_: e.activation(out=de_[:, :], in_=pc_[:, 4:8], func=AF.Exp), reads=[pc_], writes=[de_])
                        for g in range(2):
                            K.op("pe", lambda e, g=g: e.matmul(pG[:, g * 128:(g + 1) * 128], lhsT=BT_bf[:, g, cs], rhs=CT_bf[:, g, cs], start=True, stop=True),
                                 reads=[BT_bf, CT_bf], writes=[pG], inc=(g == 1))
                        for h in range(4):
                            ta_, E_, at_, pr_ = ta.next(), Eb.next(), attT.next(), pr.next()
                            K.op("pool", lambda e, ta_=ta_, h=h: e.tensor_scalar(out=ta_[:, :], in0=triM, scalar1=aT[:, c, 4 * d + h:4 * d + h + 1], scalar2=None,
                                                                               op0=ALU.mult), reads=[tri, aT], writes=[ta_])
                            K.op("pe", lambda e, ta_=ta_, pr_=pr_: e.matmul(pr_[:, :], lhsT=ones_f[:, :], rhs=ta_[:, :], start=True, stop=False),
                                 reads=[ones_f, ta_], writes=[pr_], inc=False)
                            K.op("pe", lambda e, pr_=pr_: e.matmul(pr_[:, :], lhsT=ident_f[:, :], rhs=negM, start=False, stop=True),
                                 reads=[ident_f, tri], writes=[pr_])
                            K.op("act", lambda e, pr_=pr_, E_=E_, h=h, nc_=nc_: e.activation(out=E_[:, :], in_=pr_[:, :], func=AF.Exp, bias=nc_[:, h:h + 1], scale=1.0),
                                 reads=[pr_, nc_], writes=[E_])
                            g = h // 2
                            K.op("dve", lambda e, E_=E_, at_=at_, h=h, g=g: e.scalar_tensor_tensor(
                                out=at_[:, :], in0=E_[:, :], scalar=dtT[:, c, 4 * d + h:4 * d + h + 1], in1=pG[:, g * 128:(g + 1) * 128],
                                op0=ALU.mult, op1=ALU.mult), reads=[E_, dtT, pG], writes=[at_])
                            K.op("pe", lambda e, at_=at_, h=h: e.matmul(pY[:, h * 64:(h + 1) * 64], lhsT=at_[:, :], rhs=xtok_bf[:, c, h * 64:(h + 1) * 64],
                                                                      start=True, stop=True), reads=[at_, xtok_bf], writes=[pY])
                        xw_ = xw.next()
                        for h in range(4):
                            K.op("dve", lambda e, xw_=xw_, h=h, we_=we_: e.tensor_scalar(out=xw_[:, h, :], in0=xtok[:, c, h * 64:(h + 1) * 64],
                                                                                      scalar1=we_[:, h:h + 1], scalar2=None, op0=ALU.mult),
                                 reads=[xtok, we_, xw_], writes=[xw_])
                        for h in range(4):
                            g = h // 2
                            K.op("pe", lambda e, xw_=xw_, h=h, g=g: e.matmul(pS[:, h * 64:(h + 1) * 64], lhsT=Btok_bf[:, c, g * 128:(g + 1) * 128],
                                                                           rhs=xw_[:, h, :], start=True, stop=True),
                                 reads=[Btok_bf, xw_], writes=[pS], inc=(h == 3))
                        for h in range(4):
                            g = h // 2
                            K.op("pe", lambda e, h=h, g=g: e.matmul(pI[:, h * 64:(h + 1) * 64], lhsT=CT_bf[:, g, cs], rhs=hst_bf[:, h, :],
                                                                  start=True, stop=True), reads=[CT_bf, hst_bf], writes=[pI], inc=(h == 3))
                        if d == 1:
                            for h in range(4):
                                K.op("dve", lambda e, h=h, ec_=ec_: e.tensor_scalar(out=ybk[:, c, h * 64:(h + 1) * 64], in0=pI[:, h * 64:(h + 1) * 64],
                                                                                 scalar1=ec_[:, h:h + 1], scalar2=None, op0=ALU.mult),
                                     reads=[pI, ec_, ybk], writes=[ybk])
                            K.op("dve", lambda e: e.tensor_tensor(out=ybk[:, c, :], in0=ybk[:, c, :], in1=pY[:, :], op=ALU.add),
                                 reads=[ybk, pY], writes=[ybk])
                        else:
                            ac_ = accb.next()
                            for h in range(4):
                                K.op("dve", lambda e, h=h, ec_=ec_, ac_=ac_: e.scalar_tensor_tensor(
                                    out=ac_[:, h * 64:(h + 1) * 64], in0=pI[:, h * 64:(h + 1) * 64], scalar=ec_[:, h:h + 1],
                                    in1=ybk[:, c, h * 64:(h + 1) * 64], op0=ALU.mult, op1=ALU.add), reads=[pI, ec_, ybk, ac_], writes=[ac_])
                            K.op("dve", lambda e, ac_=ac_: e.tensor_tensor(out=ac_[:, :], in0=ac_[:, :], in1=pY[:, :], op=ALU.add),
                                 reads=[ac_, pY], writes=[ac_])
                            if not (last and c >= 32):
                                for h in range(4):
                                    K.op("dve", lambda e, h=h, ac_=ac_: e.scalar_tensor_tensor(
                                        out=ac_[:, h * 64:(h + 1) * 64], in0=xtok[:, c, h * 64:(h + 1) * 64], scalar=dcol[:, h:h + 1],
                                        in1=ac_[:, h * 64:(h + 1) * 64], op0=ALU.mult, op1=ALU.add), reads=[xtok, dcol, ac_], writes=[ac_])
                                z_, yo_ = zb.next(), yo.next()
                                K.dma("sp", z_[:, :], ZT[c * 128:(c + 1) * 128, :], writes=[z_])
                                K.op("act", lambda e, z_=z_: e.activation(out=z_[:, :], in_=z_[:, :], func=AF.Silu), reads=[z_], writes=[z_])
                                K.op("pool", lambda e, z_=z_, yo_=yo_, ac_=ac_: e.tensor_tensor(out=yo_[:, :], in0=ac_[:, :], in1=z_[:, :], op=ALU.mult),
                                     reads=[ac_, z_], writes=[yo_])
                                K.dma("sp", YM[c * 128:(c + 1) * 128, 256:512], yo_[:, :], reads=[yo_])
                        for h in range(4):
                            K.op("dve", lambda e, h=h, de_=de_: e.scalar_tensor_tensor(out=hst[:, h, :], in0=hst[:, h, :], scalar=de_[:, h:h + 1],
                                                                                     in1=pS[:, h * 64:(h + 1) * 64], op0=ALU.mult, op1=ALU.add),
                                 reads=[hst, de_, pS], writes=[hst])
                        K.op("act", lambda e: e.copy(out=hst_bf[:, :, :], in_=hst[:, :, :]), reads=[hst], writes=[hst_bf])
        K.barrier()

    nl = cfg.get("layers", DEPTH)
    phases = cfg.get("phases")

    def on(name):
        return phases is None or name in phases

    pairs = []
    for l in range(nl):
        pairs += [(w1in[l], w1in_h[l]), (w1out[l], w1out_h[l]), (wmix[l], wmix_h[l]), (wmo[l], wmo_h[l]),
                  (w2in[l], w2in_h[l]), (w2out[l], w2out_h[l])]
    if on("convert"):
        convert(pairs)
    if on("mod"):
        modulation()
    bufA, bufB = xs_a, xs_b
    for l in range(nl):
        last = (l == DEPTH - 1)
        if l == 0:
            segs = [(x_in, bufA[0:S], S, 0), (ctx_in, bufA[S:T], NCTX, 1)]
        else:
            segs = [(bufB[0:S], bufA[0:S], S, 0), (bufB[S:T], bufA[S:T], NCTX, 1)]
        if on("ffn1"):
            ffn_phase(l, 0, w1in_h[l], w1out_h[l], segs)
        if on("inproj"):
            inproj_phase(l, bufA, [(0, S, 0), (S, NCTX, 1)])
        if on("attA"):
            attention_phase(l, "A", not last)
        if on("attC"):
            attention_phase(l, "C", not last)
        if on("attD"):
            attention_phase(l, "D", not last)
        if on("ssd"):
            ssd_phase(l, last)
        msegs = [(0, S, 0)] + ([] if last else [(S, NCTX, 1)])
        if on("merge"):
            merge_phase(l, bufA, bufB, msegs)
        if last:
            segs = [(bufB[0:S], out, S, 0)]
        else:
            segs = [(bufB[0:S], bufA[0:S], S, 0), (bufB[S:T], bufA[S:T], NCTX, 1)]
        if on("ffn2"):
            ffn_phase(l, 2, w2in_h[l], w2out_h[l], segs)
        bufA, bufB = bufB, bufA

    for name in cfg.get("dump", []):
        src = {"xs_a": xs_a, "xs_b": xs_b, "YM": YM}[name]
        dout = nc.dram_tensor("dbg_" + name, [T, D], F32, kind="ExternalOutput").ap()
        with contextlib.ExitStack() as st:
            C2 = Ctx(nc, st)
            tb = Rot([C2.sb([128, D], F32, "dbt") for _ in range(2)])
            for i in range(T // 128):
                t_ = tb.next()
                K.dma("sp", t_[:, :], src[i * 128:(i + 1) * 128, :], writes=[t_])
                K.dma("sp", dout[i * 128:(i + 1) * 128, :], t_[:, :], reads=[t_])
        K.barrier()
    K.barrier()


def _mix_cols():
    def heads(base, hs):
        r = []
        for h in hs:
            r.extend(range(base + h * 64, base + (h + 1) * 64))
        return r

    def sw(cols):
        return [(c - (c % 64)) + ((c % 64) ^ 16) if False else c for c in cols]

    def swap(base, cols):
        return [base + ((c - base) // 64) * 64 + (((c - base) % 64) ^ 16) for c in cols]

    cols = []
    aqA, aqB = heads(0, [0, 2]), heads(0, [1, 3])
    ak = list(range(256, 384))
    cols += aqA + aqB + swap(0, aqA) + swap(0, aqB) + ak + swap(256, ak)
    cqA, cqB = heads(1544, [0, 2]), heads(1544, [1, 3])
    ck = list(range(1800, 1928))
    cols += cqA + cqB + swap(1544, cqA) + swap(1544, cqB) + ck + swap(1800, ck)
    cols += list(range(2056, 2312)) + list(range(2312, 2568))
    cols += list(range(768, 1536))
    cols += list(range(1536, 1544))
    cols += list(range(384, 512)) + list(range(1928, 2056)) + list(range(2568, 2824)) + list(range(512, 768))
    assert len(cols) == 3592
    return np.asarray(cols, dtype=np.int64)


def _rope_tables():
    t = np.arange(S)
    pos = np.stack([t // 64, t % 64], -1).astype(np.float32)
    inv = (np.float32(10000.0) ** (-np.arange(16, dtype=np.float32) / np.float32(16))).astype(np.float32)
    ang = pos[:, :, None] * inv[None, None, :]
    cos = np.cos(ang).astype(np.float32)
    sin = np.sin(ang).astype(np.float32)
    cos_t = np.ones((128, T), np.float32)
    sin_t = np.zeros((128, T), np.float32)
    for p in range(128):
        d = p % 64
        a, b, i = d // 32, (d // 16) % 2, d % 16
        cos_t[p, :S] = cos[:, a, i]
        sin_t[p, :S] = sin[:, a, i] * (1.0 if b == 1 else -1.0)
    return cos_t, sin_t


def _na_table(rpb):
    out = np.full((3, 128, 8, 4, 512), -30000.0, np.float32)
    kk = np.arange(128)
    krl, kc = kk // 64, kk % 64
    qq = np.arange(512)
    qrl, qc = qq // 64, qq % 64
    cs = np.clip(qc - 8, 0, 48)
    for ty, R0 in enumerate([0, 8, 56]):
        for i in range(8):
            kr = R0 - 4 + 2 * i + krl
            qr = R0 + qrl
            rs = np.clip(qr - 4, 0, 56)
            vr = (kr[:, None] >= rs[None, :]) & (kr[:, None] < rs[None, :] + 8) & (kr[:, None] >= 0) & (kr[:, None] < 64)
            vc = (kc[:, None] >= cs[None, :]) & (kc[:, None] < cs[None, :] + 16)
            valid = vr & vc
            dr = np.clip(kr[:, None] - qr[None, :] + 7, 0, 14)
            dc = np.clip(kc[:, None] - qc[None, :] + 15, 0, 30)
            for h in range(4):
                g = rpb[h][dr, dc]
                out[ty, :, i, h, :] = np.where(valid, g, np.float32(-30000.0))
    return out


def _swa_mask():
    m = np.zeros((128, 6, 512), np.float32)
    kk = np.arange(128)[:, None]
    qq = np.arange(128)[None, :]
    for i in range(6):
        for r in range(4):
            rel = i - 1 - r
            if rel == 0:
                blk = np.ones((128, 128), np.float32)
            elif rel == -1:
                blk = (kk >= qq).astype(np.float32)
            elif rel == 1:
                blk = (kk <= qq).astype(np.float32)
            else:
                blk = np.zeros((128, 128), np.float32)
            m[:, i, r * 128:(r + 1) * 128] = blk
    return m.reshape(128, 6 * 512)


def _tri_consts():
    k = np.arange(128)[:, None]
    j = np.arange(128)[None, :]
    triU = (k <= j).astype(np.float32)
    triL = (k >= j).astype(np.float32)
    negf = np.where(k > j, np.float32(-30000.0), np.float32(0.0)).astype(np.float32)
    negb = np.where(k < j, np.float32(-30000.0), np.float32(0.0)).astype(np.float32)
    return np.ascontiguousarray(np.stack([triU, triL, negf, negb], axis=1).reshape(128, 512))


def make_in_maps(inputs, ncores=NCORES):
    perm = ffn_in_perm()
    f1 = np.ascontiguousarray(inputs["ffn1_w_in"][:, :, perm])
    f2 = np.ascontiguousarray(inputs["ffn2_w_in"][:, :, perm])
    wmix = np.ascontiguousarray(inputs["mix_w_in"][:, :, _mix_cols()])
    cos_t, sin_t = _rope_tables()
    d = np.arange(128) % 64
    qn, kn = inputs["gqa_q_norm"], inputs["gqa_k_norm"]
    qk_col = np.ascontiguousarray(np.stack([qn[:, d], qn[:, d ^ 16], kn[:, d], kn[:, d ^ 16]], axis=-1)).astype(np.float32)
    cw = inputs["ssd_conv_w"][:, :, 0, :]
    convw = np.ascontiguousarray(cw.reshape(DEPTH, 5, 6, 128).transpose(0, 3, 2, 1))
    convb = np.ascontiguousarray(inputs["ssd_conv_b"].reshape(DEPTH, 6, 128).transpose(0, 2, 1))
    natab = np.stack([_na_table(inputs["na_rpb"][l]) for l in range(DEPTH)], 0).reshape(DEPTH, 3, 128, 8 * 4 * 512)
    shared = {
        "ada_w": inputs["ada_w"], "ada_b": inputs["ada_b"], "ln_g": inputs["ln_g"], "ln_b": inputs["ln_b"],
        "ffn1_w_in": f1, "ffn1_w_out": inputs["ffn1_w_out"], "ffn2_w_in": f2, "ffn2_w_out": inputs["ffn2_w_out"],
        "wmix": wmix, "mix_w_out": inputs["mix_w_out"], "mix_norm_g": inputs["mix_norm_g"],
        "cos_t": cos_t, "sin_t": sin_t, "qk_col": qk_col, "swa_sink": inputs["swa_sink"],
        "convw": convw, "convb": convb,
        "dt_bias": np.ascontiguousarray(inputs["ssd_dt_bias"].reshape(DEPTH, 8, 1)),
        "a_log": np.ascontiguousarray(inputs["ssd_A_log"].reshape(DEPTH, 8, 1)),
        "ssd_D": inputs["ssd_D"], "natab": np.ascontiguousarray(natab), "maska": _swa_mask(), "tri_in": _tri_consts(),
    }
    shared = {k: np.ascontiguousarray(v, dtype=np.float32) for k, v in shared.items()}
    maps = []
    for b in range(ncores):
        m = dict(shared)
        m["x"] = np.ascontiguousarray(inputs["x"][b])
        m["ctx"] = np.ascontiguousarray(inputs["ctx"][b])
        m["c2t"] = np.ascontiguousarray(np.stack([inputs["c"][b], inputs["c_ctx"]], axis=1))
        maps.append(m)
    return maps


def kernel(**inputs):
    inputs = {k: np.asarray(v) for k, v in inputs.items()}
    nc = build_program({})
    maps = make_in_maps(inputs)
    res = run_bass_kernel_spmd(nc, maps, core_ids=list(range(NCORES)))
    return np.stack([r["out"] for r in res.results], axis=0).astype(np.float32)
```

```python
import contextlib
import numpy as np
import concourse.bass as bass
import concourse.mybir as mybir
from concourse.bass_utils import run_bass_kernel_spmd

F32 = mybir.dt.float32
BF16 = mybir.dt.bfloat16
AF = mybir.ActivationFunctionType
ALU = mybir.AluOpType
AX = mybir.AxisListType

D = 1024
S = 4096
NCTX = 256
T = S + NCTX
DEPTH = 2
DFF = 2816
NJ = DFF // 128
EPS = 1e-5
ALPHA = (2 * DEPTH) ** 0.25
NCORES = 8


class Res:
    __slots__ = ("w", "r", "name")

    def __init__(self, name=""):
        self.w = None
        self.r = {}
        self.name = name


class Buf:
    def __init__(self, t, name=""):
        self.t = t
        self.res = Res(name)

    def __getitem__(self, idx):
        return self.t[idx]


def _res(x):
    return x.res if isinstance(x, Buf) else x


class Tracker:
    ENG = ("pe", "act", "dve", "pool", "sp")
    NDS = 24

    def __init__(self, nc, stack):
        self.nc = nc
        self.E = {"pe": nc.tensor, "act": nc.scalar, "dve": nc.vector, "pool": nc.gpsimd, "sp": nc.sync}
        self.dkeys = {q: ["d_%s_%d" % (q, i) for i in range(self.NDS)] for q in ("sp", "pool")}
        keys = list(self.ENG) + self.dkeys["sp"] + self.dkeys["pool"]
        self.keys = keys
        self.sem = {k: stack.enter_context(nc.semaphore("s_" + k)) for k in keys}
        self.cnt = {k: 0 for k in keys}
        self.seen = {e: {k: 0 for k in keys} for e in self.ENG}
        self.pending = {e: 0 for e in self.ENG}
        self.dn = {"sp": 0, "pool": 0}

    def _needs(self, e, reads, writes):
        n = {}

        def add(k, c):
            if c > n.get(k, 0):
                n[k] = c

        for r in reads:
            r = _res(r)
            if r.w is not None:
                add(*r.w)
        for w in writes:
            w = _res(w)
            if w.w is not None and w.w[0] != e:
                add(*w.w)
            for k, c in w.r.items():
                if k != e:
                    add(k, c)
        if e == "pe":
            n.pop("pe", None)
        return n

    def _wait(self, e, needs):
        for k, c in needs.items():
            if c > self.seen[e][k]:
                self.E[e].wait_ge(self.sem[k], c)
                self.seen[e][k] = c

    def op(self, e, fn, reads=(), writes=(), inc=True):
        self._wait(e, self._needs(e, reads, writes))
        ins = fn(self.E[e])
        if inc:
            self.cnt[e] += 1
            ins.then_inc(self.sem[e], 1)
            c = self.cnt[e]
            self.pending[e] = 0
        else:
            c = self.cnt[e] + 1
            self.pending[e] += 1
        wset = set(id(_res(w)) for w in writes)
        for w in writes:
            w = _res(w)
            w.w = (e, c)
            w.r = {}
        for r in reads:
            r = _res(r)
            if id(r) not in wset:
                if c > r.r.get(e, 0):
                    r.r[e] = c
        return ins

    def dma(self, e, out, in_, reads=(), writes=(), **kw):
        k = self.dkeys[e][self.dn[e] % self.NDS]
        self.dn[e] += 1
        needs = self._needs(None, reads, writes)
        if self.cnt[k] > needs.get(k, 0):
            needs[k] = self.cnt[k]
        self._wait(e, needs)
        ins = self.E[e].dma_start(out=out, in_=in_, **kw)
        self.cnt[k] += 16
        ins.then_inc(self.sem[k], 16)
        c = self.cnt[k]
        for w in writes:
            w = _res(w)
            w.w = (k, c)
            w.r = {}
        for r in reads:
            r = _res(r)
            if c > r.r.get(k, 0):
                r.r[k] = c
        return ins

    def barrier(self):
        for e in self.ENG:
            assert self.pending[e] == 0, e
        self._wait("sp", dict(self.cnt))
        self.cnt["sp"] += 1
        self.E["sp"].nop().then_inc(self.sem["sp"], 1)
        tok = {"sp": self.cnt["sp"]}
        for e in self.ENG:
            if e != "sp":
                self._wait(e, tok)
                for k in self.keys:
                    self.seen[e][k] = max(self.seen[e][k], self.cnt[k]) if k != "sp" else self.seen[e][k]


class Rot:
    def __init__(self, bufs):
        self.bufs = bufs
        self.i = 0

    def next(self):
        b = self.bufs[self.i % len(self.bufs)]
        self.i += 1
        return b


class Ctx:
    N = [0]

    def __init__(self, nc, stack):
        self.nc = nc
        self.stack = stack

    def sb(self, shape, dt, name=None):
        Ctx.N[0] += 1
        name = (name or "sb") + "_%d" % Ctx.N[0]
        return Buf(self.stack.enter_context(self.nc.sbuf_tensor(name, list(shape), dt)), name)

    def ps(self, shape, dt, name=None):
        Ctx.N[0] += 1
        name = (name or "ps") + "_%d" % Ctx.N[0]
        return Buf(self.stack.enter_context(self.nc.psum_tensor(name, list(shape), dt)), name)

    def dram(self, name, shape, dt, kind="Internal"):
        return self.nc.dram_tensor(name, list(shape), dt, kind=kind)


def ffn_in_perm():
    idx = []
    for b in range(NJ // 2):
        for jj in range(2):
            j = 2 * b + jj
            idx.extend(range(j * 128, (j + 1) * 128))
        for jj in range(2):
            j = 2 * b + jj
            idx.extend(range(DFF + j * 128, DFF + (j + 1) * 128))
    return np.asarray(idx, dtype=np.int64)


def build_program(cfg):
    nc = bass.Bass("TRN2", target_bir_lowering=False)
    stack = contextlib.ExitStack()
    with stack:
        _build(nc, stack, cfg)
    return nc


def _build(nc, stack, cfg):
    K = Tracker(nc, stack)
    C = Ctx(nc, stack)
    stop_after = cfg.get("stop_after", "all")

    x_in = nc.dram_tensor("x", [S, D], F32, kind="ExternalInput").ap()
    ctx_in = nc.dram_tensor("ctx", [NCTX, D], F32, kind="ExternalInput").ap()
    c2t = nc.dram_tensor("c2t", [D, 2], F32, kind="ExternalInput").ap()
    ada_w = nc.dram_tensor("ada_w", [DEPTH, D, 9 * D], F32, kind="ExternalInput").ap()
    ada_b = nc.dram_tensor("ada_b", [DEPTH, 9 * D], F32, kind="ExternalInput").ap()
    ln_g = nc.dram_tensor("ln_g", [DEPTH, 3, D], F32, kind="ExternalInput").ap()
    ln_b = nc.dram_tensor("ln_b", [DEPTH, 3, D], F32, kind="ExternalInput").ap()
    w1in = nc.dram_tensor("ffn1_w_in", [DEPTH, D, 2 * DFF], F32, kind="ExternalInput").ap()
    w1out = nc.dram_tensor("ffn1_w_out", [DEPTH, DFF, D], F32, kind="ExternalInput").ap()
    w2in = nc.dram_tensor("ffn2_w_in", [DEPTH, D, 2 * DFF], F32, kind="ExternalInput").ap()
    w2out = nc.dram_tensor("ffn2_w_out", [DEPTH, DFF, D], F32, kind="ExternalInput").ap()
    out = nc.dram_tensor("out", [S, D], F32, kind="ExternalOutput").ap()

    modv = nc.dram_tensor("modv", [DEPTH, 2, 9 * D], F32, kind="Internal").ap()
    xs_a = nc.dram_tensor("xs_a", [T, D], F32, kind="Internal").ap()
    xs_b = nc.dram_tensor("xs_b", [T, D], F32, kind="Internal").ap()
    w1in_h = nc.dram_tensor("w1in_h", [DEPTH, D, 2 * DFF], BF16, kind="Internal").ap()
    w1out_h = nc.dram_tensor("w1out_h", [DEPTH, DFF, D], BF16, kind="Internal").ap()
    w2in_h = nc.dram_tensor("w2in_h", [DEPTH, D, 2 * DFF], BF16, kind="Internal").ap()
    w2out_h = nc.dram_tensor("w2out_h", [DEPTH, DFF, D], BF16, kind="Internal").ap()

    NMIX = 3592
    wmix = nc.dram_tensor("wmix", [DEPTH, D, NMIX], F32, kind="ExternalInput").ap()
    wmo = nc.dram_tensor("mix_w_out", [DEPTH, D, D], F32, kind="ExternalInput").ap()
    mixg = nc.dram_tensor("mix_norm_g", [DEPTH, D], F32, kind="ExternalInput").ap()
    cos_t = nc.dram_tensor("cos_t", [128, T], F32, kind="ExternalInput").ap()
    sin_t = nc.dram_tensor("sin_t", [128, T], F32, kind="ExternalInput").ap()
    qk_col = nc.dram_tensor("qk_col", [DEPTH, 128, 4], F32, kind="ExternalInput").ap()
    sink_in = nc.dram_tensor("swa_sink", [DEPTH, 4], F32, kind="ExternalInput").ap()
    convw = nc.dram_tensor("convw", [DEPTH, 128, 6, 5], F32, kind="ExternalInput").ap()
    convb = nc.dram_tensor("convb", [DEPTH, 128, 6], F32, kind="ExternalInput").ap()
    dtb_in = nc.dram_tensor("dt_bias", [DEPTH, 8, 1], F32, kind="ExternalInput").ap()
    alog_in = nc.dram_tensor("a_log", [DEPTH, 8, 1], F32, kind="ExternalInput").ap()
    ssdd_in = nc.dram_tensor("ssd_D", [DEPTH, 4], F32, kind="ExternalInput").ap()
    natab = nc.dram_tensor("natab", [DEPTH, 3, 128, 8 * 4 * 512], F32, kind="ExternalInput").ap()
    maska_in = nc.dram_tensor("maska", [128, 6 * 512], F32, kind="ExternalInput").ap()
    tri_in = nc.dram_tensor("tri_in", [128, 4 * 128], F32, kind="ExternalInput").ap()

    wmix_h = nc.dram_tensor("wmix_h", [DEPTH, D, NMIX], BF16, kind="Internal").ap()
    wmo_h = nc.dram_tensor("wmo_h", [DEPTH, D, D], BF16, kind="Internal").ap()
    QA = nc.dram_tensor("QA", [2, 128, T], BF16, kind="Internal").ap()
    KA = nc.dram_tensor("KA", [1, 128, T], BF16, kind="Internal").ap()
    QC = nc.dram_tensor("QC", [2, 128, T], BF16, kind="Internal").ap()
    KC = nc.dram_tensor("KC", [1, 128, T], BF16, kind="Internal").ap()
    QD = nc.dram_tensor("QD", [2, 128, T], BF16, kind="Internal").ap()
    KD = nc.dram_tensor("KD", [2, 128, T], BF16, kind="Internal").ap()
    XBCT = nc.dram_tensor("XBCT", [6, 128, T], F32, kind="Internal").ap()
    DTT = nc.dram_tensor("DTT", [8, T], F32, kind="Internal").ap()
    VT = nc.dram_tensor("VT", [T, 512], BF16, kind="Internal").ap()
    ZT = nc.dram_tensor("ZT", [T, 256], F32, kind="Internal").ap()
    YM = nc.dram_tensor("YM", [T, D], F32, kind="Internal").ap()

    ident_b = C.sb([128, 128], BF16, "identb")
    ident_f = C.sb([128, 128], F32, "identf")
    K.op("pool", lambda e: e.memset(ident_f[:, :], 0.0), writes=[ident_f])
    K.op("pool", lambda e: e.affine_select(out=ident_f[:, :], in_=ident_f[:, :], pattern=[[-1, 128]], base=0,
                                            channel_multiplier=1, compare_op=ALU.not_equal, fill=1.0),
         reads=[ident_f], writes=[ident_f])
    K.op("dve", lambda e: e.tensor_copy(out=ident_b[:, :], in_=ident_f[:, :]), reads=[ident_f], writes=[ident_b])

    epsc = C.sb([128, 1], F32, "epsc")
    onec = C.sb([128, 1], F32, "onec")
    K.op("pool", lambda e: e.memset(epsc[:, :], EPS), writes=[epsc])
    K.op("pool", lambda e: e.memset(onec[:, :], 1.0), writes=[onec])

    def convert(pairs):
        with contextlib.ExitStack() as st:
            C2 = Ctx(nc, st)
            CH = 4096
            stg = Rot([C2.sb([128, CH], F32, "cvs") for _ in range(3)])
            cvo = Rot([C2.sb([128, CH], BF16, "cvo") for _ in range(3)])
            engs = ["act", "dve", "pool"]
            i = 0
            for src, dst in pairs:
                rows, cols = src.shape
                a = rows // 128
                sv = src.rearrange("(p a) c -> p (a c)", p=128)
                dv = dst.rearrange("(p a) c -> p (a c)", p=128)
                tot = a * cols
                for o in range(0, tot, CH):
                    n = min(CH, tot - o)
                    s_, d_ = stg.next(), cvo.next()
                    K.dma("sp", s_[:, :n], sv[:, o:o + n], writes=[s_])
                    en = engs[i % 3]
                    if en == "act":
                        K.op("act", lambda e, s_=s_, d_=d_, n=n: e.copy(out=d_[:, :n], in_=s_[:, :n]), reads=[s_], writes=[d_])
                    else:
                        K.op(en, lambda e, s_=s_, d_=d_, n=n: e.tensor_copy(out=d_[:, :n], in_=s_[:, :n]), reads=[s_], writes=[d_])
                    K.dma("sp", dv[:, o:o + n], d_[:, :n], reads=[d_])
                    i += 1
        K.barrier()

    def modulation():
        with contextlib.ExitStack() as st:
            C2 = Ctx(nc, st)
            ct = C2.sb([128, 8, 2], F32, "ct")
            K.dma("sp", ct[:, :, :], c2t.rearrange("(p j) m -> p j m", p=128), writes=[ct])
            K.op("act", lambda e: e.activation(out=ct[:, :, :], in_=ct[:, :, :], func=AF.Silu), reads=[ct], writes=[ct])
            wb = Rot([C2.sb([128, 8, 512], F32, "adaw") for _ in range(3)])
            msb = C2.sb([2, 9 * D], F32, "msb")
            bsb = C2.sb([2, 9 * D], F32, "bsb")
            pm = Rot([C2.ps([128, 512], F32, "pm") for _ in range(2)])
            for l in range(DEPTH):
                K.dma("pool", bsb[:, :], ada_b[l:l + 1, :].partition_broadcast(2), writes=[bsb])
                wv = ada_w[l].rearrange("(p j) n -> p j n", p=128)
                for nb in range(18):
                    w_ = wb.next()
                    K.dma("sp", w_[:, :, :], wv[:, :, nb * 512:(nb + 1) * 512], writes=[w_])
                    p_ = pm.next()
                    for j in range(8):
                        K.op("pe", lambda e, j=j, w_=w_, p_=p_: e.matmul(p_[0:2, :], lhsT=ct[:, j, :], rhs=w_[:, j, :],
                                                                      start=(j == 0), stop=(j == 7)),
                             reads=[ct, w_], writes=[p_], inc=(j == 7))
                    K.op("dve", lambda e, p_=p_, nb=nb: e.tensor_tensor(out=msb[:, nb * 512:(nb + 1) * 512], in0=p_[0:2, :],
                                                                     in1=bsb[:, nb * 512:(nb + 1) * 512], op=ALU.add),
                         reads=[p_, bsb], writes=[msb])
                K.dma("sp", modv[l], msb[:, :], reads=[msb])
        K.barrier()

    def ffn_phase(l, sub, w_in_h, w_out_h, segs):
        with contextlib.ExitStack() as st:
            C2 = Ctx(nc, st)
            w2 = C2.sb([128, NJ, D], BF16, "w2")
            K.dma("pool", w2[:, :, :], w_out_h.rearrange("(j p) n -> p j n", p=128), writes=[w2])
            lng = C2.sb([128, D], F32, "lng")
            lnb = C2.sb([128, D], F32, "lnb")
            K.dma("pool", lng[:, :], ln_g[l, sub:sub + 1, :].partition_broadcast(128), writes=[lng])
            K.dma("pool", lnb[:, :], ln_b[l, sub:sub + 1, :].partition_broadcast(128), writes=[lnb])
            shift = C2.sb([128, D], F32, "shift")
            sc1p = C2.sb([128, D], F32, "sc1p")
            gateh = C2.sb([128, D], F32, "gateh")
            xin = Rot([C2.sb([128, D], F32, "xin") for _ in range(2)])
            xr = Rot([C2.sb([128, D], F32, "xr") for _ in range(4)])
            hb = Rot([C2.sb([128, D], BF16, "hb") for _ in range(2)])
            hT = Rot([C2.sb([128, 8, 512], BF16, "hT") for _ in range(2)])
            gT = C2.sb([128, NJ, 512], BF16, "gT")
            w1 = Rot([C2.sb([128, 8, 512], BF16, "w1") for _ in range(3)])
            sa = Rot([C2.sb([128, 512], F32, "sa") for _ in range(2)])
            ybuf = Rot([C2.sb([128, D], F32, "ybuf") for _ in range(2)])
            obuf = Rot([C2.sb([128, D], F32, "obuf") for _ in range(2)])
            stats = Rot([C2.sb([128, 2, 6], F32, "stats") for _ in range(2)])
            mv = Rot([C2.sb([128, 2], F32, "mv") for _ in range(2)])
            rstd = Rot([C2.sb([128, 1], F32, "rstd") for _ in range(2)])
            mhalf = C2.sb([128, 1], F32, "mhalf")
            K.op("pool", lambda e: e.memset(mhalf[:, :], -0.5), writes=[mhalf])
            pT = C2.ps([128, D], BF16, "pT")
            psA = Rot([C2.ps([128, 512], F32, "psA") for _ in range(2)])
            psU = Rot([C2.ps([128, 512], F32, "psU") for _ in range(2)])
            pO = C2.ps([128, D], F32, "pO")
            w1v = w_in_h.rearrange("(kc p) f -> p kc f", p=128)
            mbase = 3 * sub
            def prologue(src, g0, G, hT_):
                for t in range(G // 128):
                    r0 = g0 + t * 128
                    xi = xin.next()
                    K.dma("sp", xi[:, :], src[r0:r0 + 128, :], writes=[xi])
                    K.op("pool", lambda e, xi=xi: e.tensor_tensor(out=xi[:, :], in0=xi[:, :], in1=sc1p[:, :], op=ALU.mult),
                         reads=[xi, sc1p], writes=[xi])
                    hb_ = hb.next()
                    K.op("dve", lambda e, xi=xi, hb_=hb_: e.tensor_tensor(out=hb_[:, :], in0=xi[:, :], in1=shift[:, :], op=ALU.add),
                         reads=[xi, shift], writes=[hb_])
                    for kc in range(8):
                        K.op("pe", lambda e, kc=kc, hb_=hb_: e.transpose(out=pT[:, kc * 128:(kc + 1) * 128],
                                                                       in_=hb_[:, kc * 128:(kc + 1) * 128], identity=ident_b[:, :]),
                             reads=[hb_, ident_b], writes=[pT], inc=(kc == 7))
                    K.op("act", lambda e, t=t, hT_=hT_: e.copy(out=hT_[:, :, t * 128:(t + 1) * 128],
                                                             in_=pT[:, :].rearrange("p (k n) -> p k n", k=8)),
                         reads=[pT], writes=[hT_])

            def first_mm(G, hT_):
                for b in range(NJ // 2):
                    w_ = w1.next()
                    K.dma("sp", w_[:, :, :], w1v[:, :, b * 512:(b + 1) * 512], writes=[w_])
                    for jj in range(2):
                        j = 2 * b + jj
                        A_, U_ = psA.next(), psU.next()
                        for kc in range(8):
                            K.op("pe", lambda e, kc=kc, w_=w_, A_=A_, jj=jj: e.matmul(
                                A_[:, :G], lhsT=w_[:, kc, jj * 128:(jj + 1) * 128], rhs=hT_[:, kc, :G],
                                start=(kc == 0), stop=(kc == 7)), reads=[w_, hT_], writes=[A_], inc=(kc == 7))
                        for kc in range(8):
                            K.op("pe", lambda e, kc=kc, w_=w_, U_=U_, jj=jj: e.matmul(
                                U_[:, :G], lhsT=w_[:, kc, 256 + jj * 128:256 + (jj + 1) * 128], rhs=hT_[:, kc, :G],
                                start=(kc == 0), stop=(kc == 7)), reads=[w_, hT_], writes=[U_], inc=(kc == 7))
                        sa_ = sa.next()
                        K.op("act", lambda e, A_=A_, sa_=sa_: e.activation(out=sa_[:, :G], in_=A_[:, :G], func=AF.Silu),
                             reads=[A_], writes=[sa_])
                        K.op("dve", lambda e, U_=U_, sa_=sa_, j=j: e.tensor_tensor(out=gT[:, j, :G], in0=sa_[:, :G], in1=U_[:, :G],
                                                                                 op=ALU.mult),
                             reads=[sa_, U_], writes=[gT])

            def second_mm(src, dst, g0, G):
                nt = G // 128
                xrs = []
                for t in range(nt):
                    xr_ = xr.next()
                    K.dma("sp", xr_[:, :], src[g0 + t * 128:g0 + (t + 1) * 128, :], writes=[xr_])
                    xrs.append(xr_)
                for t in range(nt):
                    r0 = g0 + t * 128
                    for nh in range(2):
                        for j in range(NJ):
                            K.op("pe", lambda e, j=j, nh=nh, t=t: e.matmul(
                                pO[:, nh * 512:(nh + 1) * 512], lhsT=gT[:, j, t * 128:(t + 1) * 128],
                                rhs=w2[:, j, nh * 512:(nh + 1) * 512], start=(j == 0), stop=(j == NJ - 1)),
                                reads=[gT, w2], writes=[pO], inc=(j == NJ - 1 and nh == 1))
                    xr_ = xrs[t]
                    y_ = ybuf.next()
                    K.op("dve", lambda e, y_=y_: e.tensor_tensor(out=y_[:, :], in0=pO[:, :], in1=gateh[:, :], op=ALU.mult),
                         reads=[pO, gateh], writes=[y_])
                    K.op("dve", lambda e, y_=y_, xr_=xr_: e.scalar_tensor_tensor(out=y_[:, :], in0=xr_[:, :], scalar=ALPHA, in1=y_[:, :],
                                                                               op0=ALU.mult, op1=ALU.add),
                         reads=[xr_, y_], writes=[y_])
                    ln_epilogue(C2, y_, lng, lnb, stats.next(), mv.next(), rstd.next(), mhalf, obuf, dst[r0:r0 + 128, :])

            for (src, dst, ntok, mrow) in segs:
                K.dma("sp", shift[:, :], modv[l, mrow:mrow + 1, (mbase + 0) * D:(mbase + 1) * D].partition_broadcast(128), writes=[shift])
                K.dma("sp", sc1p[:, :], modv[l, mrow:mrow + 1, (mbase + 1) * D:(mbase + 2) * D].partition_broadcast(128), writes=[sc1p])
                K.dma("sp", gateh[:, :], modv[l, mrow:mrow + 1, (mbase + 2) * D:(mbase + 3) * D].partition_broadcast(128), writes=[gateh])
                K.op("pool", lambda e: e.tensor_scalar(out=sc1p[:, :], in0=sc1p[:, :], scalar1=1.0, scalar2=None, op0=ALU.add),
                     reads=[sc1p], writes=[sc1p])
                K.op("pool", lambda e: e.tensor_scalar(out=gateh[:, :], in0=gateh[:, :], scalar1=0.5, scalar2=None, op0=ALU.mult),
                     reads=[gateh], writes=[gateh])
                groups = [(g0, min(512, ntok - g0)) for g0 in range(0, ntok, 512)]
                hTs = [hT.next() for _ in groups]
                prologue(src, groups[0][0], groups[0][1], hTs[0])
                for gi, (g0, G) in enumerate(groups):
                    first_mm(G, hTs[gi])
                    if gi + 1 < len(groups):
                        prologue(src, groups[gi + 1][0], groups[gi + 1][1], hTs[gi + 1])
                    second_mm(src, dst, g0, G)
        K.barrier()

    def ln_epilogue(C2, y_, lng, lnb, st_, mv_, rs_, mhalf, obuf, dst_ap):
        for hh in range(2):
            K.op("dve", lambda e, hh=hh: e.bn_stats(out=st_[:, hh, :], in_=y_[:, hh * 512:(hh + 1) * 512]), reads=[y_], writes=[st_])
        K.op("dve", lambda e: e.bn_aggr(out=mv_[:, :], in_=st_[:, :, :].rearrange("p a b -> p (a b)")), reads=[st_], writes=[mv_])
        K.op("pool", lambda e: e.tensor_scalar(out=rs_[:, :], in0=mv_[:, 1:2], scalar1=EPS, scalar2=None, op0=ALU.add),
             reads=[mv_], writes=[rs_])
        K.op("pool", lambda e: e.tensor_tensor(out=rs_[:, :], in0=rs_[:, :], in1=mhalf[:, :], op=ALU.pow), reads=[rs_, mhalf], writes=[rs_])
        K.op("dve", lambda e: e.tensor_scalar(out=y_[:, :], in0=y_[:, :], scalar1=mv_[:, 0:1], scalar2=rs_[:, 0:1],
                                              op0=ALU.subtract, op1=ALU.mult), reads=[y_, mv_, rs_], writes=[y_])
        K.op("pool", lambda e: e.tensor_tensor(out=y_[:, :], in0=y_[:, :], in1=lng[:, :], op=ALU.mult), reads=[y_, lng], writes=[y_])
        o_ = obuf.next()
        K.op("pool", lambda e: e.tensor_tensor(out=o_[:, :], in0=y_[:, :], in1=lnb[:, :], op=ALU.add), reads=[y_, lnb], writes=[o_])
        K.dma("pool", dst_ap, o_[:, :], reads=[o_])

    def inproj_phase(l, src, segs):
        with contextlib.ExitStack() as st:
            C2 = Ctx(nc, st)
            wm = C2.sb([128, 8, NMIX], BF16, "wm")
            wv = wmix_h[l].rearrange("(kc p) f -> p kc f", p=128)
            for kc in range(8):
                K.dma("pool", wm[:, kc, :], wv[:, kc, :], writes=[wm])
            cosT = C2.sb([128, T], F32, "cosT")
            sinT = C2.sb([128, T], F32, "sinT")
            K.dma("pool", cosT[:, :], cos_t, writes=[cosT])
            K.dma("pool", sinT[:, :], sin_t, writes=[sinT])
            gcol = C2.sb([128, 4], F32, "gcol")
            K.dma("pool", gcol[:, :], qk_col[l], writes=[gcol])
            bones = C2.sb([128, 128], F32, "bones")
            K.op("pool", lambda e: e.memset(bones[:, :], 0.0), writes=[bones])
            K.op("pool", lambda e: e.memset(bones[0:64, 0:64], 1.0), reads=[bones], writes=[bones])
            K.op("pool", lambda e: e.memset(bones[64:128, 64:128], 1.0), reads=[bones], writes=[bones])
            shift = C2.sb([128, D], F32, "shift")
            sc1p = C2.sb([128, D], F32, "sc1p")
            xin = Rot([C2.sb([128, D], F32, "xin") for _ in range(2)])
            hb = Rot([C2.sb([128, D], BF16, "hb") for _ in range(2)])
            hT = Rot([C2.sb([128, 8, 512], BF16, "hT") for _ in range(2)])
            t1 = Rot([C2.sb([128, 512], F32, "t1") for _ in range(2)])
            t2 = Rot([C2.sb([128, 512], F32, "t2") for _ in range(2)])
            sq = Rot([C2.sb([128, 512], F32, "sq") for _ in range(2)])
            rs = Rot([C2.sb([128, 512], F32, "rs") for _ in range(2)])
            ob = Rot([C2.sb([128, 512], BF16, "ob") for _ in range(3)])
            of = Rot([C2.sb([128, 512], F32, "of") for _ in range(3)])
            vtb = Rot([C2.sb([128, 512], BF16, "vtb") for _ in range(2)])
            ztb = Rot([C2.sb([128, 256], F32, "ztb") for _ in range(2)])
            pT = C2.ps([128, D], BF16, "pT")
            psF = Rot([C2.ps([128, 512], F32, "psF") for _ in range(4)])
            pss = C2.ps([128, 512], F32, "pss")
            pTM = C2.ps([128, 1024], F32, "pTM")

            def fm_tile(i, hT_, G):
                M = 128 if i < 22 else 8
                p_ = psF.next()
                for kc in range(8):
                    K.op("pe", lambda e, kc=kc, p_=p_: e.matmul(p_[0:M, :G], lhsT=wm[:, kc, i * 128:i * 128 + M], rhs=hT_[:, kc, :G],
                                                               start=(kc == 0), stop=(kc == 7)),
                         reads=[wm, hT_], writes=[p_], inc=(kc == 7))
                return p_

            for (tok0, ntok, mrow) in segs:
                K.dma("pool", shift[:, :], modv[l, mrow:mrow + 1, 3 * D:4 * D].partition_broadcast(128), writes=[shift])
                K.dma("pool", sc1p[:, :], modv[l, mrow:mrow + 1, 4 * D:5 * D].partition_broadcast(128), writes=[sc1p])
                K.op("pool", lambda e: e.tensor_scalar(out=sc1p[:, :], in0=sc1p[:, :], scalar1=1.0, scalar2=None, op0=ALU.add),
                     reads=[sc1p], writes=[sc1p])
                def prologue(c0, G, hT_):
                    for t in range(G // 128):
                        r0 = c0 + t * 128
                        xi = xin.next()
                        K.dma("pool", xi[:, :], src[r0:r0 + 128, :], writes=[xi])
                        K.op("pool", lambda e, xi=xi: e.tensor_tensor(out=xi[:, :], in0=xi[:, :], in1=sc1p[:, :], op=ALU.mult),
                             reads=[xi, sc1p], writes=[xi])
                        hb_ = hb.next()
                        K.op("dve", lambda e, xi=xi, hb_=hb_: e.tensor_tensor(out=hb_[:, :], in0=xi[:, :], in1=shift[:, :], op=ALU.add),
                             reads=[xi, shift], writes=[hb_])
                        for kc in range(8):
                            K.op("pe", lambda e, kc=kc, hb_=hb_: e.transpose(out=pT[:, kc * 128:(kc + 1) * 128],
                                                                           in_=hb_[:, kc * 128:(kc + 1) * 128], identity=ident_b[:, :]),
                                 reads=[hb_, ident_b], writes=[pT], inc=(kc == 7))
                        K.op("act", lambda e, t=t, hT_=hT_: e.copy(out=hT_[:, :, t * 128:(t + 1) * 128],
                                                                 in_=pT[:, :].rearrange("p (k n) -> p k n", k=8)),
                             reads=[pT], writes=[hT_])

                groups = [(g0, min(512, ntok - g0)) for g0 in range(0, ntok, 512)]
                hTs = [hT.next() for _ in groups]
                prologue(tok0 + groups[0][0], groups[0][1], hTs[0])
                for gi, (g0, G) in enumerate(groups):
                    nt = G // 128
                    c0 = tok0 + g0
                    hT_ = hTs[gi]
                    plan = [(0, 2, QA[0], None), (1, 3, QA[1], None), (4, 5, KA[0], None),
                            (6, 8, QC[0], 0), (7, 9, QC[1], 0), (10, 11, KC[0], 2)]
                    for (io, isw, dst, ncol) in plan:
                        po = fm_tile(io, hT_, G)
                        psw = fm_tile(isw, hT_, G)
                        t1_, t2_, o_ = t1.next(), t2.next(), ob.next()
                        if ncol is None:
                            K.op("dve", lambda e, po=po, t1_=t1_: e.tensor_tensor(out=t1_[:, :G], in0=po[:, :G], in1=cosT[:, c0:c0 + G], op=ALU.mult),
                                 reads=[po, cosT], writes=[t1_])
                            K.op("dve", lambda e, psw=psw, t2_=t2_: e.tensor_tensor(out=t2_[:, :G], in0=psw[:, :G], in1=sinT[:, c0:c0 + G], op=ALU.mult),
                                 reads=[psw, sinT], writes=[t2_])
                            K.op("pool", lambda e, t1_=t1_, t2_=t2_, o_=o_: e.tensor_tensor(out=o_[:, :G], in0=t1_[:, :G], in1=t2_[:, :G], op=ALU.add),
                                 reads=[t1_, t2_], writes=[o_])
                        else:
                            sq_, rs_ = sq.next(), rs.next()
                            K.op("act", lambda e, po=po, sq_=sq_: e.activation(out=sq_[:, :G], in_=po[:, :G], func=AF.Square),
                                 reads=[po], writes=[sq_])
                            K.op("pe", lambda e, sq_=sq_: e.matmul(pss[:, :G], lhsT=bones[:, :], rhs=sq_[:, :G], start=True, stop=True),
                                 reads=[bones, sq_], writes=[pss])
                            K.op("act", lambda e, sq_=sq_: e.activation(out=sq_[:, :G], in_=pss[:, :G], func=AF.Sqrt, bias=epsc[:, 0:1], scale=1.0 / 64),
                                 reads=[pss, epsc], writes=[sq_])
                            K.op("dve", lambda e, sq_=sq_, rs_=rs_: e.reciprocal(out=rs_[:, :G], in_=sq_[:, :G]), reads=[sq_], writes=[rs_])
                            K.op("dve", lambda e, po=po, t1_=t1_: e.scalar_tensor_tensor(out=t1_[:, :G], in0=po[:, :G], scalar=gcol[:, ncol:ncol + 1],
                                                                                       in1=cosT[:, c0:c0 + G], op0=ALU.mult, op1=ALU.mult),
                                 reads=[po, cosT, gcol], writes=[t1_])
                            K.op("dve", lambda e, psw=psw, t2_=t2_: e.scalar_tensor_tensor(out=t2_[:, :G], in0=psw[:, :G], scalar=gcol[:, ncol + 1:ncol + 2],
                                                                                         in1=sinT[:, c0:c0 + G], op0=ALU.mult, op1=ALU.mult),
                                 reads=[psw, sinT, gcol], writes=[t2_])
                            K.op("pool", lambda e, t1_=t1_, t2_=t2_: e.tensor_tensor(out=t1_[:, :G], in0=t1_[:, :G], in1=t2_[:, :G], op=ALU.add),
                                 reads=[t1_, t2_], writes=[t1_])
                            K.op("pool", lambda e, t1_=t1_, rs_=rs_, o_=o_: e.tensor_tensor(out=o_[:, :G], in0=t1_[:, :G], in1=rs_[:, :G], op=ALU.mult),
                                 reads=[t1_, rs_], writes=[o_])
                        K.dma("sp", dst[:, c0:c0 + G], o_[:, :G], reads=[o_])
                    if gi + 1 < len(groups):
                        prologue(tok0 + groups[gi + 1][0], groups[gi + 1][1], hTs[gi + 1])
                    for (i, dst) in [(12, QD[0]), (13, QD[1]), (14, KD[0]), (15, KD[1])]:
                        p_ = fm_tile(i, hT_, G)
                        o_ = ob.next()
                        K.op("act", lambda e, p_=p_, o_=o_: e.copy(out=o_[:, :G], in_=p_[:, :G]), reads=[p_], writes=[o_])
                        K.dma("sp", dst[:, c0:c0 + G], o_[:, :G], reads=[o_])
                    for i in range(16, 23):
                        M = 128 if i < 22 else 8
                        p_ = fm_tile(i, hT_, G)
                        o_ = of.next()
                        K.op("act", lambda e, p_=p_, o_=o_, M=M: e.copy(out=o_[0:M, :G], in_=p_[0:M, :G]), reads=[p_], writes=[o_])
                        dst = XBCT[i - 16] if i < 22 else DTT
                        K.dma("sp", dst[0:M, c0:c0 + G], o_[0:M, :G], reads=[o_])
                    for t in range(nt):
                        r0 = c0 + t * 128
                        for (n0, n1) in [(0, 512), (512, 768)]:
                            for kc in range(8):
                                K.op("pe", lambda e, kc=kc, t=t, n0=n0, n1=n1: e.matmul(
                                    pTM[:, n0:n1], lhsT=hT_[:, kc, t * 128:(t + 1) * 128], rhs=wm[:, kc, 22 * 128 + 8 + n0:22 * 128 + 8 + n1],
                                    start=(kc == 0), stop=(kc == 7)), reads=[wm, hT_], writes=[pTM], inc=(kc == 7 and n0 == 512))
                        v_, z_ = vtb.next(), ztb.next()
                        K.op("act", lambda e, v_=v_: e.copy(out=v_[:, :], in_=pTM[:, 0:512]), reads=[pTM], writes=[v_])
                        K.op("dve", lambda e, z_=z_: e.tensor_copy(out=z_[:, :], in_=pTM[:, 512:768]), reads=[pTM], writes=[z_])
                        K.dma("sp", VT[r0:r0 + 128, :], v_[:, :], reads=[v_])
                        K.dma("sp", ZT[r0:r0 + 128, :], z_[:, :], reads=[z_])
        K.barrier()

    def attention_phase(l, kind, with_ctx_q):
        with contextlib.ExitStack() as st:
            C2 = Ctx(nc, st)
            Qsrc = {"A": QA, "C": QC, "D": QD}[kind]
            Ksrc = {"A": KA, "C": KC, "D": KD}[kind]
            nkt = 2 if kind == "D" else 1
            nkv = 4 if kind == "D" else 2
            vcol0 = {"A": 0, "C": 128, "D": 256}[kind]
            ycol0 = {"A": 0, "C": 512, "D": 768}[kind]
            QT = C2.sb([128, 2, T], BF16, "QT")
            KT = C2.sb([128, nkt, T], BF16, "KT")
            for j in range(2):
                K.dma("sp", QT[:, j, :], Qsrc[j], writes=[QT])
            for j in range(nkt):
                K.dma("sp", KT[:, j, :], Ksrc[j], writes=[KT])
            vraw = C2.sb([128, 34, nkv * 64], BF16, "vraw")
            K.dma("pool", vraw[:, :, :], VT[:, vcol0:vcol0 + nkv * 64].rearrange("(c p) f -> p c f", p=128), writes=[vraw])
            Vp = C2.sb([128, 34, nkv, 65], BF16, "Vp")
            K.op("pool", lambda e: e.memset(Vp[:, :, :, :], 1.0), writes=[Vp])
            for g in range(nkv):
                K.op("dve", lambda e, g=g: e.tensor_copy(out=Vp[:, :, g, 0:64], in_=vraw[:, :, g * 64:(g + 1) * 64]),
                     reads=[vraw, Vp], writes=[Vp])
            esink = None
            if kind == "A":
                esink = C2.sb([128, 4], F32, "esink")
                K.dma("pool", esink[:, :], sink_in[l:l + 1, :].partition_broadcast(128), writes=[esink])
                K.op("act", lambda e: e.activation(out=esink[:, :], in_=esink[:, :], func=AF.Exp), reads=[esink], writes=[esink])
                mf = C2.sb([128, 6 * 512], F32, "mf")
                K.dma("pool", mf[:, :], maska_in, writes=[mf])
                maskb = C2.sb([128, 6, 512], BF16, "maskb")
                K.op("dve", lambda e: e.tensor_copy(out=maskb[:, :, :], in_=mf[:, :].rearrange("p (a b) -> p a b", a=6)), reads=[mf], writes=[maskb])
            if kind == "D":
                tab = C2.sb([128, 8, 4, 512], F32, "natab")
            PT = Rot([C2.sb([128, 512], BF16, "PT") for _ in range(5)])
            sbias = Rot([C2.sb([128, 512], F32, "sbias") for _ in range(3)])
            oT = Rot([C2.sb([128, 512], F32, "oT") for _ in range(2)])
            rden = Rot([C2.sb([128, 4], F32, "rden") for _ in range(2)])
            ytile = Rot([C2.sb([128, 4, 256], F32, "ytile") for _ in range(3)])
            psS = Rot([C2.ps([128, 512], F32, "psS") for _ in range(4)])
            pOut = Rot([C2.ps([128, 512], F32, "pOut") for _ in range(2)])
            pTr = C2.ps([128, 4, 128], F32, "pTr")

            blocks = [(s * 512, 512, s) for s in range(8)]
            if with_ctx_q:
                blocks.append((S, NCTX, None))
            units = []
            for (q0, G, s) in blocks:
                yt_ = ytile.next()
                for h in range(4):
                    if kind == "D":
                        jt, po, kt, g = h // 2, (h % 2) * 64, h // 2, h
                    else:
                        jt, po, kt, g = h % 2, (h // 2) * 64, 0, h // 2
                    if s is None:
                        chunks = [(32, None), (33, None)]
                    elif kind == "C":
                        chunks = [(c, None) for c in range(34)]
                    elif kind == "A":
                        chunks = [(4 * s - 1 + i, i) for i in range(6) if 0 <= 4 * s - 1 + i < 32] + [(32, None), (33, None)]
                    else:
                        chunks = [(4 * s - 2 + i, i) for i in range(8) if 0 <= 4 * s - 2 + i < 32] + [(32, None), (33, None)]
                    units.append(dict(q0=q0, G=G, s=s, h=h, jt=jt, po=po, kt=kt, g=g, chunks=chunks, yt=yt_, po_=None))
            items = [(u, ci) for u in units for ci in range(len(u["chunks"]))]
            LA = 2
            pts = {}
            state = {"tab": None}

            def s_stage(k):
                u, ci = items[k]
                c, mi = u["chunks"][ci]
                G, q0, po, kt, jt, h = u["G"], u["q0"], u["po"], u["kt"], u["jt"], u["h"]
                if kind == "D" and u["s"] is not None:
                    ttype = 0 if u["s"] == 0 else (2 if u["s"] == 7 else 1)
                    if ttype != state["tab"]:
                        K.dma("sp", tab[:, :, :, :].rearrange("p a h q -> p (a h q)"), natab[l, ttype], writes=[tab])
                        state["tab"] = ttype
                ps_ = psS.next()
                K.op("pe", lambda e: e.matmul(ps_[:, :G], lhsT=KT[po:po + 64, kt, c * 128:(c + 1) * 128],
                                             rhs=QT[po:po + 64, jt, q0:q0 + G], start=True, stop=True),
                     reads=[KT, QT], writes=[ps_])
                pt_ = PT.next()
                if kind == "D" and mi is not None:
                    sb_ = sbias.next()
                    K.op("dve", lambda e: e.scalar_tensor_tensor(out=sb_[:, :G], in0=ps_[:, :G], scalar=0.125, in1=tab[:, mi, h, :G],
                                                                 op0=ALU.mult, op1=ALU.add), reads=[ps_, tab], writes=[sb_])
                    K.op("act", lambda e: e.activation(out=pt_[:, :G], in_=sb_[:, :G], func=AF.Exp), reads=[sb_], writes=[pt_])
                else:
                    K.op("act", lambda e: e.activation(out=pt_[:, :G], in_=ps_[:, :G], func=AF.Exp, scale=0.125), reads=[ps_], writes=[pt_])
                    if kind == "A" and mi is not None:
                        K.op("pool", lambda e: e.tensor_tensor(out=pt_[:, :G], in0=pt_[:, :G], in1=maskb[:, mi, :G], op=ALU.mult),
                             reads=[pt_, maskb], writes=[pt_])
                pts[k] = pt_

            def pv_stage(k):
                u, ci = items[k]
                c, mi = u["chunks"][ci]
                G, g = u["G"], u["g"]
                if ci == 0:
                    u["po_"] = pOut.next()
                po_ = u["po_"]
                pt_ = pts.pop(k)
                n = len(u["chunks"])
                K.op("pe", lambda e: e.matmul(po_[0:65, :G], lhsT=Vp[:, c, g, :], rhs=pt_[:, :G], start=(ci == 0), stop=(ci == n - 1)),
                     reads=[Vp, pt_], writes=[po_])

            def finalize(u):
                G, q0, h, yt_, po_ = u["G"], u["q0"], u["h"], u["yt"], u["po_"]
                nt = G // 128
                o_ = oT.next()
                K.op("act", lambda e: e.copy(out=o_[0:65, :G], in_=po_[0:65, :G]), reads=[po_], writes=[o_])
                for qt in range(nt):
                    K.op("pe", lambda e, qt=qt: e.transpose(out=pTr[:, qt, 0:65], in_=o_[0:65, qt * 128:(qt + 1) * 128],
                                                           identity=ident_f[0:65, 0:65]),
                         reads=[o_, ident_f], writes=[pTr], inc=(qt == nt - 1))
                rd_ = rden.next()
                if kind == "A":
                    K.op("dve", lambda e: e.tensor_scalar(out=rd_[:, 0:nt], in0=pTr[:, 0:nt, 64], scalar1=esink[:, h:h + 1], scalar2=None,
                                                          op0=ALU.add), reads=[pTr, esink], writes=[rd_])
                    K.op("dve", lambda e: e.reciprocal(out=rd_[:, 0:nt], in_=rd_[:, 0:nt]), reads=[rd_], writes=[rd_])
                else:
                    K.op("dve", lambda e: e.reciprocal(out=rd_[:, 0:nt], in_=pTr[:, 0:nt, 64]), reads=[pTr], writes=[rd_])
                for qt in range(nt):
                    K.op("dve", lambda e, qt=qt: e.tensor_scalar(out=yt_[:, qt, h * 64:(h + 1) * 64], in0=pTr[:, qt, 0:64],
                                                                scalar1=rd_[:, qt:qt + 1], scalar2=None, op0=ALU.mult),
                         reads=[pTr, rd_, yt_], writes=[yt_])
                if h == 3:
                    for qt in range(nt):
                        K.dma("sp", YM[q0 + qt * 128:q0 + (qt + 1) * 128, ycol0:ycol0 + 256], yt_[:, qt, :], reads=[yt_])

            NI = len(items)
            for k in range(min(LA, NI)):
                s_stage(k)
            pend = []
            for k in range(NI):
                if k + LA < NI:
                    s_stage(k + LA)
                pv_stage(k)
                for (kk, u) in list(pend):
                    if k >= kk:
                        finalize(u)
                        pend.remove((kk, u))
                u, ci = items[k]
                if ci == len(u["chunks"]) - 1:
                    pend.append((k + 2, u))
            for (kk, u) in pend:
                finalize(u)
        K.barrier()

    def merge_phase(l, src, dst, segs):
        with contextlib.ExitStack() as st:
            C2 = Ctx(nc, st)
            wo = C2.sb([128, 8, D], BF16, "wo")
            K.dma("pool", wo[:, :, :], wmo_h[l].rearrange("(kc p) n -> p kc n", p=128), writes=[wo])
            lng = C2.sb([128, D], F32, "lng")
            lnb = C2.sb([128, D], F32, "lnb")
            gm = C2.sb([128, D], F32, "gm")
            K.dma("pool", lng[:, :], ln_g[l, 1:2, :].partition_broadcast(128), writes=[lng])
            K.dma("pool", lnb[:, :], ln_b[l, 1:2, :].partition_broadcast(128), writes=[lnb])
            K.dma("pool", gm[:, :], mixg[l:l + 1, :].partition_broadcast(128), writes=[gm])
            gate = C2.sb([128, D], F32, "gate")
            yin = Rot([C2.sb([128, D], F32, "yin") for _ in range(3)])
            junk = C2.sb([128, 256], F32, "junk")
            ss = Rot([C2.sb([128, 4], F32, "ss") for _ in range(2)])
            yb = Rot([C2.sb([128, D], BF16, "yb") for _ in range(2)])
            yT = Rot([C2.sb([128, 8, 128], BF16, "yT") for _ in range(3)])
            xr = Rot([C2.sb([128, D], F32, "xr") for _ in range(3)])
            ybuf = Rot([C2.sb([128, D], F32, "ybuf") for _ in range(2)])
            obuf = Rot([C2.sb([128, D], F32, "obuf") for _ in range(2)])
            stats = Rot([C2.sb([128, 2, 6], F32, "stats") for _ in range(2)])
            mv = Rot([C2.sb([128, 2], F32, "mv") for _ in range(2)])
            rstd = Rot([C2.sb([128, 1], F32, "rstd") for _ in range(2)])
            mhalf = C2.sb([128, 4], F32, "mhalf")
            K.op("pool", lambda e: e.memset(mhalf[:, :], -0.5), writes=[mhalf])
            mhalf1 = C2.sb([128, 1], F32, "mhalf1")
            K.op("pool", lambda e: e.memset(mhalf1[:, :], -0.5), writes=[mhalf1])
            pT = C2.ps([128, D], BF16, "pT")
            pO = Rot([C2.ps([128, D], F32, "pO") for _ in range(2)])
            def stage1(r0):
                y_ = yin.next()
                K.dma("sp", y_[:, :], YM[r0:r0 + 128, :], writes=[y_])
                ss_ = ss.next()
                for g in range(4):
                    K.op("act", lambda e, g=g: e.activation(out=junk[:, :], in_=y_[:, g * 256:(g + 1) * 256], func=AF.Square,
                                                           accum_out=ss_[:, g:g + 1]), reads=[y_], writes=[junk, ss_])
                K.op("pool", lambda e: e.tensor_scalar(out=ss_[:, :], in0=ss_[:, :], scalar1=1.0 / 256, scalar2=EPS, op0=ALU.mult, op1=ALU.add),
                     reads=[ss_], writes=[ss_])
                K.op("pool", lambda e: e.tensor_tensor(out=ss_[:, :], in0=ss_[:, :], in1=mhalf[:, :], op=ALU.pow), reads=[ss_, mhalf], writes=[ss_])
                for g in range(4):
                    K.op("dve", lambda e, g=g: e.tensor_scalar(out=y_[:, g * 256:(g + 1) * 256], in0=y_[:, g * 256:(g + 1) * 256],
                                                              scalar1=ss_[:, g:g + 1], scalar2=None, op0=ALU.mult),
                         reads=[y_, ss_], writes=[y_])
                yb_ = yb.next()
                K.op("pool", lambda e: e.tensor_tensor(out=yb_[:, :], in0=y_[:, :], in1=gm[:, :], op=ALU.mult), reads=[y_, gm], writes=[yb_])
                for kc in range(8):
                    K.op("pe", lambda e, kc=kc: e.transpose(out=pT[:, kc * 128:(kc + 1) * 128], in_=yb_[:, kc * 128:(kc + 1) * 128],
                                                           identity=ident_b[:, :]), reads=[yb_, ident_b], writes=[pT], inc=(kc == 7))
                yT_ = yT.next()
                K.op("act", lambda e: e.copy(out=yT_[:, :, :], in_=pT[:, :].rearrange("p (k n) -> p k n", k=8)), reads=[pT], writes=[yT_])
                xr_ = xr.next()
                K.dma("sp", xr_[:, :], src[r0:r0 + 128, :], writes=[xr_])
                return (yT_, xr_)

            def stage2(r0, yT_, xr_):
                pO_ = pO.next()
                for nh in range(2):
                    for kc in range(8):
                        K.op("pe", lambda e, kc=kc, nh=nh: e.matmul(pO_[:, nh * 512:(nh + 1) * 512], lhsT=yT_[:, kc, :],
                                                                   rhs=wo[:, kc, nh * 512:(nh + 1) * 512], start=(kc == 0), stop=(kc == 7)),
                             reads=[yT_, wo], writes=[pO_], inc=(kc == 7 and nh == 1))
                o_ = ybuf.next()
                K.op("dve", lambda e: e.tensor_tensor(out=o_[:, :], in0=pO_[:, :], in1=gate[:, :], op=ALU.mult),
                     reads=[pO_, gate], writes=[o_])
                K.op("dve", lambda e: e.scalar_tensor_tensor(out=o_[:, :], in0=xr_[:, :], scalar=ALPHA, in1=o_[:, :],
                                                             op0=ALU.mult, op1=ALU.add), reads=[xr_, o_], writes=[o_])
                ln_epilogue(C2, o_, lng, lnb, stats.next(), mv.next(), rstd.next(), mhalf1, obuf, dst[r0:r0 + 128, :])

            for (tok0, ntok, mrow) in segs:
                K.dma("sp", gate[:, :], modv[l, mrow:mrow + 1, 5 * D:6 * D].partition_broadcast(128), writes=[gate])
                rows = list(range(tok0, tok0 + ntok, 128))
                cur = stage1(rows[0])
                for ti, r0 in enumerate(rows):
                    nxt = stage1(rows[ti + 1]) if ti + 1 < len(rows) else None
                    stage2(r0, *cur)
                    cur = nxt
        K.barrier()

    def ssd_phase(l, last):
        with contextlib.ExitStack() as st:
            C2 = Ctx(nc, st)
            xtok = C2.sb([128, 34, 256], F32, "xtok")
            xtok_bf = C2.sb([128, 34, 256], BF16, "xtokb")
            Btok_bf = C2.sb([128, 34, 256], BF16, "Btokb")
            BT_bf = C2.sb([128, 2, T], BF16, "BTb")
            CT_bf = C2.sb([128, 2, T], BF16, "CTb")
            dtT = C2.sb([128, 34, 8], F32, "dtT")
            aT = C2.sb([128, 34, 8], F32, "aT")
            tri = C2.sb([128, 4, 128], F32, "tri")
            K.dma("pool", tri[:, :, :].rearrange("p a b -> p (a b)"), tri_in, writes=[tri])
            ones_f = C2.sb([128, 128], F32, "onesf")
            K.op("pool", lambda e: e.memset(ones_f[:, :], 1.0), writes=[ones_f])
            dcol = C2.sb([128, 4], F32, "dcol")
            K.dma("pool", dcol[:, :], ssdd_in[l:l + 1, :].partition_broadcast(128), writes=[dcol])
            with contextlib.ExitStack() as st2:
                C3 = Ctx(nc, st2)
                cw = C3.sb([128, 6, 5], F32, "cw")
                cb = C3.sb([128, 6], F32, "cb")
                K.dma("pool", cw[:, :, :], convw[l], writes=[cw])
                K.dma("pool", cb[:, :], convb[l], writes=[cb])
                xi_r = Rot([C3.sb([128, T], F32, "cxi") for _ in range(2)])
                acc_r = Rot([C3.sb([128, T], F32, "cacc") for _ in range(1)])
                uf = Rot([C3.sb([128, T], F32, "cuf") for _ in range(1)])
                ptf = Rot([C3.ps([128, 4, 128], F32, "ptf") for _ in range(2)])
                ptb = Rot([C3.ps([128, 8, 128], BF16, "ptb") for _ in range(2)])
                for i in range(6):
                    xi, acc = xi_r.next(), acc_r.next()
                    K.dma("sp", xi[:, :], XBCT[i], writes=[xi])
                    K.op("dve", lambda e, xi=xi, acc=acc, i=i: e.tensor_scalar(out=acc[:, :], in0=xi[:, :], scalar1=cw[:, i, 2:3], scalar2=cb[:, i:i + 1],
                                                                             op0=ALU.mult, op1=ALU.add), reads=[xi, cw, cb], writes=[acc])
                    for k in (0, 1, 3, 4):
                        s_ = k - 2
                        for (lo, hi) in [(0, S), (S, T)]:
                            olo = max(lo, lo - s_)
                            ohi = min(hi, hi - s_)
                            K.op("dve", lambda e, xi=xi, acc=acc, i=i, k=k, olo=olo, ohi=ohi, s_=s_: e.scalar_tensor_tensor(
                                out=acc[:, olo:ohi], in0=xi[:, olo + s_:ohi + s_], scalar=cw[:, i, k:k + 1], in1=acc[:, olo:ohi],
                                op0=ALU.mult, op1=ALU.add), reads=[xi, acc, cw], writes=[acc])
                    if i < 2:
                        u_ = uf.next()
                        K.op("act", lambda e, acc=acc, u_=u_: e.activation(out=u_[:, :], in_=acc[:, :], func=AF.Silu), reads=[acc], writes=[u_])
                        for c0 in range(0, 34, 4):
                            n = min(4, 34 - c0)
                            p_ = ptf.next()
                            for cc in range(n):
                                c = c0 + cc
                                K.op("pe", lambda e, p_=p_, u_=u_, c=c, cc=cc: e.transpose(out=p_[:, cc, :], in_=u_[:, c * 128:(c + 1) * 128],
                                                                                         identity=ident_f[:, :]),
                                     reads=[u_, ident_f], writes=[p_], inc=(cc == n - 1))
                            K.op("act", lambda e, p_=p_, c0=c0, n=n, i=i: e.copy(out=xtok[:, c0:c0 + n, i * 128:(i + 1) * 128], in_=p_[:, 0:n, :]),
                                 reads=[p_], writes=[xtok])
                    elif i < 4:
                        g = i - 2
                        K.op("act", lambda e, acc=acc, g=g: e.activation(out=BT_bf[:, g, :], in_=acc[:, :], func=AF.Silu), reads=[acc], writes=[BT_bf])
                        for c0 in range(0, 34, 8):
                            n = min(8, 34 - c0)
                            p_ = ptb.next()
                            for cc in range(n):
                                c = c0 + cc
                                K.op("pe", lambda e, p_=p_, g=g, c=c, cc=cc: e.transpose(out=p_[:, cc, :], in_=BT_bf[:, g, c * 128:(c + 1) * 128],
                                                                                       identity=ident_b[:, :]),
                                     reads=[BT_bf, ident_b], writes=[p_], inc=(cc == n - 1))
                            K.op("dve", lambda e, p_=p_, c0=c0, n=n, g=g: e.tensor_copy(out=Btok_bf[:, c0:c0 + n, g * 128:(g + 1) * 128], in_=p_[:, 0:n, :]),
                                 reads=[p_], writes=[Btok_bf])
                    else:
                        g = i - 4
                        K.op("act", lambda e, acc=acc, g=g: e.activation(out=CT_bf[:, g, :], in_=acc[:, :], func=AF.Silu), reads=[acc], writes=[CT_bf])
                K.op("dve", lambda e: e.tensor_copy(out=xtok_bf[:, :, :], in_=xtok[:, :, :]), reads=[xtok], writes=[xtok_bf])
                K.barrier()
            with contextlib.ExitStack() as st2:
                C3 = Ctx(nc, st2)
                pdt = C3.ps([128, 34, 8], F32, "pdt")
                pat = C3.ps([128, 34, 8], F32, "pat")
                dtr = C3.sb([8, T], F32, "dtr")
                ar = C3.sb([8, T], F32, "ar")
                dtb = C3.sb([8, 1], F32, "dtb")
                alog = C3.sb([8, 1], F32, "alog")
                K.dma("sp", dtr[:, :], DTT, writes=[dtr])
                K.dma("sp", dtb[:, :], dtb_in[l], writes=[dtb])
                K.dma("sp", alog[:, :], alog_in[l], writes=[alog])
                K.op("act", lambda e: e.activation(out=dtr[:, :], in_=dtr[:, :], func=AF.Exp, bias=dtb[:, 0:1], scale=1.0), reads=[dtr, dtb], writes=[dtr])
                K.op("act", lambda e: e.activation(out=dtr[:, :], in_=dtr[:, :], func=AF.Ln, bias=onec[0:8, 0:1], scale=1.0), reads=[dtr, onec], writes=[dtr])
                K.op("act", lambda e: e.activation(out=alog[:, :], in_=alog[:, :], func=AF.Exp), reads=[alog], writes=[alog])
                K.op("dve", lambda e: e.tensor_scalar(out=alog[:, :], in0=alog[:, :], scalar1=-1.0, scalar2=None, op0=ALU.mult), reads=[alog], writes=[alog])
                K.op("dve", lambda e: e.tensor_scalar(out=ar[:, :], in0=dtr[:, :], scalar1=alog[:, 0:1], scalar2=None, op0=ALU.mult),
                     reads=[dtr, alog], writes=[ar])
                for c in range(34):
                    K.op("pe", lambda e, c=c: e.transpose(out=pdt[:, c, :], in_=dtr[0:8, c * 128:(c + 1) * 128], identity=ident_f[0:8, 0:8]),
                         reads=[dtr, ident_f], writes=[pdt], inc=(c == 33))
                for c in range(34):
                    K.op("pe", lambda e, c=c: e.transpose(out=pat[:, c, :], in_=ar[0:8, c * 128:(c + 1) * 128], identity=ident_f[0:8, 0:8]),
                         reads=[ar, ident_f], writes=[pat], inc=(c == 33))
                K.op("dve", lambda e: e.tensor_copy(out=dtT[:, :, :], in_=pdt[:, :, :]), reads=[pdt], writes=[dtT])
                K.op("dve", lambda e: e.tensor_copy(out=aT[:, :, :], in_=pat[:, :, :]), reads=[pat], writes=[aT])
                K.barrier()
            with contextlib.ExitStack() as st2:
                C3 = Ctx(nc, st2)
                ybk = C3.sb([128, 34, 256], F32, "ybk")
                hst = C3.sb([128, 4, 64], F32, "hst")
                hst_bf = C3.sb([128, 4, 64], BF16, "hstb")
                ncum = Rot([C3.sb([128, 4], F32, "ncum") for _ in range(2)])
                ecum = Rot([C3.sb([128, 4], F32, "ecum") for _ in range(2)])
                wend = Rot([C3.sb([128, 4], F32, "wend") for _ in range(2)])
                decay = Rot([C3.sb([128, 4], F32, "decay") for _ in range(2)])
                ta = Rot([C3.sb([128, 128], F32, "ta") for _ in range(3)])
                Eb = Rot([C3.sb([128, 128], F32, "Eb") for _ in range(3)])
                attT = Rot([C3.sb([128, 128], BF16, "attT") for _ in range(3)])
                xw = Rot([C3.sb([128, 4, 64], BF16, "xw") for _ in range(2)])
                accb = Rot([C3.sb([128, 256], F32, "accb") for _ in range(2)])
                zb = Rot([C3.sb([128, 256], F32, "zb") for _ in range(2)])
                yo = Rot([C3.sb([128, 256], F32, "yo") for _ in range(2)])
                pcs = Rot([C3.ps([128, 8], F32, "pcs") for _ in range(1)])
                pG = C3.ps([128, 256], F32, "pG")
                pI = C3.ps([128, 256], F32, "pI")
                pY = C3.ps([128, 256], F32, "pY")
                pS = C3.ps([128, 256], F32, "pS")
                pr = Rot([C3.ps([128, 128], F32, "pr") for _ in range(2)])
                for d in (1, 0):
                    K.op("dve", lambda e: e.memset(hst[:, :, :], 0.0), reads=[hst], writes=[hst])
                    K.op("dve", lambda e: e.memset(hst_bf[:, :, :], 0.0), reads=[hst_bf], writes=[hst_bf])
                    order = [33, 32] + list(range(31, -1, -1)) if d == 1 else [32, 33] + list(range(32))
                    triM = tri[:, 0, :] if d == 0 else tri[:, 1, :]
                    negM = tri[:, 2, :] if d == 0 else tri[:, 3, :]
                    for c in order:
                        cs = slice(c * 128, (c + 1) * 128)
                        pc_ = pcs.next()
                        K.op("pe", lambda e, pc_=pc_: e.matmul(pc_[:, 0:4], lhsT=triM, rhs=aT[:, c, 4 * d:4 * d + 4], start=True, stop=True),
                             reads=[tri, aT], writes=[pc_])
                        K.op("pe", lambda e, pc_=pc_: e.matmul(pc_[:, 4:8], lhsT=ones_f[:, :], rhs=aT[:, c, 4 * d:4 * d + 4], start=True, stop=True),
                             reads=[ones_f, aT], writes=[pc_])
                        nc_, ec_, we_, de_ = ncum.next(), ecum.next(), wend.next(), decay.next()
                        K.op("dve", lambda e, pc_=pc_, nc_=nc_: e.tensor_scalar(out=nc_[:, :], in0=pc_[:, 0:4], scalar1=-1.0, scalar2=None, op0=ALU.mult),
                             reads=[pc_], writes=[nc_])
                        K.op("act", lambda e, pc_=pc_, ec_=ec_: e.activation(out=ec_[:, :], in_=pc_[:, 0:4], func=AF.Exp), reads=[pc_], writes=[ec_])
                        K.op("dve", lambda e, pc_=pc_, nc_=nc_, we_=we_: e.tensor_tensor(out=we_[:, :], in0=pc_[:, 4:8], in1=nc_[:, :], op=ALU.add),
                             reads=[pc_, nc_], writes=[we_])
                        K.op("act", lambda e, we_=we_: e.activation(out=we_[:, :], in_=we_[:, :], func=AF.Exp), reads=[we_], writes=[we_])
                        K.op("dve", lambda e, we_=we_: e.tensor_tensor(out=we_[:, :], in0=we_[:, :], in1=dtT[:, c, 4 * d:4 * d + 4], op=ALU.mult),
                             reads=[we_, dtT], writes=[we_])
                        K.op("act", lambda e, pc_=pc_, de_=de_: e.activation(out=de_[:, :], in_=pc_[:, 4:8], func=AF.Exp), reads=[pc_], writes=[de_])
                        for g in range(2):
                            K.op("pe", lambda e, g=g: e.matmul(pG[:, g * 128:(g + 1) * 128], lhsT=BT_bf[:, g, cs], rhs=CT_bf[:, g, cs], start=True, stop=True),
                                 reads=[BT_bf, CT_bf], writes=[pG], inc=(g == 1))
                        for h in range(4):
                            ta_, E_, at_, pr_ = ta.next(), Eb.next(), attT.next(), pr.next()
                            K.op("pool", lambda e, ta_=ta_, h=h: e.tensor_scalar(out=ta_[:, :], in0=triM, scalar1=aT[:, c, 4 * d + h:4 * d + h + 1], scalar2=None,
                                                                               op0=ALU.mult), reads=[tri, aT], writes=[ta_])
                            K.op("pe", lambda e, ta_=ta_, pr_=pr_: e.matmul(pr_[:, :], lhsT=ones_f[:, :], rhs=ta_[:, :], start=True, stop=False),
                                 reads=[ones_f, ta_], writes=[pr_], inc=False)
                            K.op("pe", lambda e, pr_=pr_: e.matmul(pr_[:, :], lhsT=ident_f[:, :], rhs=negM, start=False, stop=True),
                                 reads=[ident_f, tri], writes=[pr_])
                            K.op("act", lambda e, pr_=pr_, E_=E_, h=h, nc_=nc_: e.activation(out=E_[:, :], in_=pr_[:, :], func=AF.Exp, bias=nc_[:, h:h + 1], scale=1.0),
                                 reads=[pr_, nc_], writes=[E_])
                            g = h // 2
                            K.op("dve", lambda e, E_=E_, at_=at_, h=h, g=g: e.scalar_tensor_tensor(
                                out=at_[:, :], in0=E_[:, :], scalar=dtT[:, c, 4 * d + h:4 * d + h + 1], in1=pG[:, g * 128:(g + 1) * 128],
                                op0=ALU.mult, op1=ALU.mult), reads=[E_, dtT, pG], writes=[at_])
                            K.op("pe", lambda e, at_=at_, h=h: e.matmul(pY[:, h * 64:(h + 1) * 64], lhsT=at_[:, :], rhs=xtok_bf[:, c, h * 64:(h + 1) * 64],
                                                                      start=True, stop=True), reads=[at_, xtok_bf], writes=[pY])
                        xw_ = xw.next()
                        for h in range(4):
                            K.op("dve", lambda e, xw_=xw_, h=h, we_=we_: e.tensor_scalar(out=xw_[:, h, :], in0=xtok[:, c, h * 64:(h + 1) * 64],
                                                                                      scalar1=we_[:, h:h + 1], scalar2=None, op0=ALU.mult),
                                 reads=[xtok, we_, xw_], writes=[xw_])
                        for h in range(4):
                            g = h // 2
                            K.op("pe", lambda e, xw_=xw_, h=h, g=g: e.matmul(pS[:, h * 64:(h + 1) * 64], lhsT=Btok_bf[:, c, g * 128:(g + 1) * 128],
                                                                           rhs=xw_[:, h, :], start=True, stop=True),
                                 reads=[Btok_bf, xw_], writes=[pS], inc=(h == 3))
                        for h in range(4):
                            g = h // 2
                            K.op("pe", lambda e, h=h, g=g: e.matmul(pI[:, h * 64:(h + 1) * 64], lhsT=CT_bf[:, g, cs], rhs=hst_bf[:, h, :],
                                                                  start=True, stop=True), reads=[CT_bf, hst_bf], writes=[pI], inc=(h == 3))
                        if d == 1:
                            for h in range(4):
                                K.op("dve", lambda e, h=h, ec_=ec_: e.tensor_scalar(out=ybk[:, c, h * 64:(h + 1) * 64], in0=pI[:, h * 64:(h + 1) * 64],
                                                                                 scalar1=ec_[:, h:h + 1], scalar2=None, op0=ALU.mult),
                                     reads=[pI, ec_, ybk], writes=[ybk])
                            K.op("dve", lambda e: e.tensor_tensor(out=ybk[:, c, :], in0=ybk[:, c, :], in1=pY[:, :], op=ALU.add),
                                 reads=[ybk, pY], writes=[ybk])
                        else:
                            ac_ = accb.next()
                            for h in range(4):
                                K.op("dve", lambda e, h=h, ec_=ec_, ac_=ac_: e.scalar_tensor_tensor(
                                    out=ac_[:, h * 64:(h + 1) * 64], in0=pI[:, h * 64:(h + 1) * 64], scalar=ec_[:, h:h + 1],
                                    in1=ybk[:, c, h * 64:(h + 1) * 64], op0=ALU.mult, op1=ALU.add), reads=[pI, ec_, ybk, ac_], writes=[ac_])
                            K.op("dve", lambda e, ac_=ac_: e.tensor_tensor(out=ac_[:, :], in0=ac_[:, :], in1=pY[:, :], op=ALU.add),
                                 reads=[ac_, pY], writes=[ac_])
                            if not (last and c >= 32):
                                for h in range(4):
                                    K.op("dve", lambda e, h=h, ac_=ac_: e.scalar_tensor_tensor(
                                        out=ac_[:, h * 64:(h + 1) * 64], in0=xtok[:, c, h * 64:(h + 1) * 64], scalar=dcol[:, h:h + 1],
                                        in1=ac_[:, h * 64:(h + 1) * 64], op0=ALU.mult, op1=ALU.add), reads=[xtok, dcol, ac_], writes=[ac_])
                                z_, yo_ = zb.next(), yo.next()
                                K.dma("sp", z_[:, :], ZT[c * 128:(c + 1) * 128, :], writes=[z_])
                                K.op("act", lambda e, z_=z_: e.activation(out=z_[:, :], in_=z_[:, :], func=AF.Silu), reads=[z_], writes=[z_])
                                K.op("pool", lambda e, z_=z_, yo_=yo_, ac_=ac_: e.tensor_tensor(out=yo_[:, :], in0=ac_[:, :], in1=z_[:, :], op=ALU.mult),
                                     reads=[ac_, z_], writes=[yo_])
                                K.dma("sp", YM[c * 128:(c + 1) * 128, 256:512], yo_[:, :], reads=[yo_])
                        for h in range(4):
                            K.op("dve", lambda e, h=h, de_=de_: e.scalar_tensor_tensor(out=hst[:, h, :], in0=hst[:, h, :], scalar=de_[:, h:h + 1],
                                                                                     in1=pS[:, h * 64:(h + 1) * 64], op0=ALU.mult, op1=ALU.add),
                                 reads=[hst, de_, pS], writes=[hst])
                        K.op("act", lambda e: e.copy(out=hst_bf[:, :, :], in_=hst[:, :, :]), reads=[hst], writes=[hst_bf])
        K.barrier()

    nl = cfg.get("layers", DEPTH)
    phases = cfg.get("phases")

    def on(name):
        return phases is None or name in phases

    pairs = []
    for l in range(nl):
        pairs += [(w1in[l], w1in_h[l]), (w1out[l], w1out_h[l]), (wmix[l], wmix_h[l]), (wmo[l], wmo_h[l]),
                  (w2in[l], w2in_h[l]), (w2out[l], w2out_h[l])]
    if on("convert"):
        convert(pairs)
    if on("mod"):
        modulation()
    bufA, bufB = xs_a, xs_b
    for l in range(nl):
        last = (l == DEPTH - 1)
        if l == 0:
            segs = [(x_in, bufA[0:S], S, 0), (ctx_in, bufA[S:T], NCTX, 1)]
        else:
            segs = [(bufB[0:S], bufA[0:S], S, 0), (bufB[S:T], bufA[S:T], NCTX, 1)]
        if on("ffn1"):
            ffn_phase(l, 0, w1in_h[l], w1out_h[l], segs)
        if on("inproj"):
            inproj_phase(l, bufA, [(0, S, 0), (S, NCTX, 1)])
        if on("attA"):
            attention_phase(l, "A", not last)
        if on("attC"):
            attention_phase(l, "C", not last)
        if on("attD"):
            attention_phase(l, "D", not last)
        if on("ssd"):
            ssd_phase(l, last)
        msegs = [(0, S, 0)] + ([] if last else [(S, NCTX, 1)])
        if on("merge"):
            merge_phase(l, bufA, bufB, msegs)
        if last:
            segs = [(bufB[0:S], out, S, 0)]
        else:
            segs = [(bufB[0:S], bufA[0:S], S, 0), (bufB[S:T], bufA[S:T], NCTX, 1)]
        if on("ffn2"):
            ffn_phase(l, 2, w2in_h[l], w2out_h[l], segs)
        bufA, bufB = bufB, bufA

    for name in cfg.get("dump", []):
        src = {"xs_a": xs_a, "xs_b": xs_b, "YM": YM}[name]
        dout = nc.dram_tensor("dbg_" + name, [T, D], F32, kind="ExternalOutput").ap()
        with contextlib.ExitStack() as st:
            C2 = Ctx(nc, st)
            tb = Rot([C2.sb([128, D], F32, "dbt") for _ in range(2)])
            for i in range(T // 128):
                t_ = tb.next()
                K.dma("sp", t_[:, :], src[i * 128:(i + 1) * 128, :], writes=[t_])
                K.dma("sp", dout[i * 128:(i + 1) * 128, :], t_[:, :], reads=[t_])
        K.barrier()
    K.barrier()


def _mix_cols():
    def heads(base, hs):
        r = []
        for h in hs:
            r.extend(range(base + h * 64, base + (h + 1) * 64))
        return r

    def sw(cols):
        return [(c - (c % 64)) + ((c % 64) ^ 16) if False else c for c in cols]

    def swap(base, cols):
        return [base + ((c - base) // 64) * 64 + (((c - base) % 64) ^ 16) for c in cols]

    cols = []
    aqA, aqB = heads(0, [0, 2]), heads(0, [1, 3])
    ak = list(range(256, 384))
    cols += aqA + aqB + swap(0, aqA) + swap(0, aqB) + ak + swap(256, ak)
    cqA, cqB = heads(1544, [0, 2]), heads(1544, [1, 3])
    ck = list(range(1800, 1928))
    cols += cqA + cqB + swap(1544, cqA) + swap(1544, cqB) + ck + swap(1800, ck)
    cols += list(range(2056, 2312)) + list(range(2312, 2568))
    cols += list(range(768, 1536))
    cols += list(range(1536, 1544))
    cols += list(range(384, 512)) + list(range(1928, 2056)) + list(range(2568, 2824)) + list(range(512, 768))
    assert len(cols) == 3592
    return np.asarray(cols, dtype=np.int64)


def _rope_tables():
    t = np.arange(S)
    pos = np.stack([t // 64, t % 64], -1).astype(np.float32)
    inv = (np.float32(10000.0) ** (-np.arange(16, dtype=np.float32) / np.float32(16))).astype(np.float32)
    ang = pos[:, :, None] * inv[None, None, :]
    cos = np.cos(ang).astype(np.float32)
    sin = np.sin(ang).astype(np.float32)
    cos_t = np.ones((128, T), np.float32)
    sin_t = np.zeros((128, T), np.float32)
    for p in range(128):
        d = p % 64
        a, b, i = d // 32, (d // 16) % 2, d % 16
        cos_t[p, :S] = cos[:, a, i]
        sin_t[p, :S] = sin[:, a, i] * (1.0 if b == 1 else -1.0)
    return cos_t, sin_t


def _na_table(rpb):
    out = np.full((3, 128, 8, 4, 512), -30000.0, np.float32)
    kk = np.arange(128)
    krl, kc = kk // 64, kk % 64
    qq = np.arange(512)
    qrl, qc = qq // 64, qq % 64
    cs = np.clip(qc - 8, 0, 48)
    for ty, R0 in enumerate([0, 8, 56]):
        for i in range(8):
            kr = R0 - 4 + 2 * i + krl
            qr = R0 + qrl
            rs = np.clip(qr - 4, 0, 56)
            vr = (kr[:, None] >= rs[None, :]) & (kr[:, None] < rs[None, :] + 8) & (kr[:, None] >= 0) & (kr[:, None] < 64)
            vc = (kc[:, None] >= cs[None, :]) & (kc[:, None] < cs[None, :] + 16)
            valid = vr & vc
            dr = np.clip(kr[:, None] - qr[None, :] + 7, 0, 14)
            dc = np.clip(kc[:, None] - qc[None, :] + 15, 0, 30)
            for h in range(4):
                g = rpb[h][dr, dc]
                out[ty, :, i, h, :] = np.where(valid, g, np.float32(-30000.0))
    return out


def _swa_mask():
    m = np.zeros((128, 6, 512), np.float32)
    kk = np.arange(128)[:, None]
    qq = np.arange(128)[None, :]
    for i in range(6):
        for r in range(4):
            rel = i - 1 - r
            if rel == 0:
                blk = np.ones((128, 128), np.float32)
            elif rel == -1:
                blk = (kk >= qq).astype(np.float32)
            elif rel == 1:
                blk = (kk <= qq).astype(np.float32)
            else:
                blk = np.zeros((128, 128), np.float32)
            m[:, i, r * 128:(r + 1) * 128] = blk
    return m.reshape(128, 6 * 512)


def _tri_consts():
    k = np.arange(128)[:, None]
    j = np.arange(128)[None, :]
    triU = (k <= j).astype(np.float32)
    triL = (k >= j).astype(np.float32)
    negf = np.where(k > j, np.float32(-30000.0), np.float32(0.0)).astype(np.float32)
    negb = np.where(k < j, np.float32(-30000.0), np.float32(0.0)).astype(np.float32)
    return np.ascontiguousarray(np.stack([triU, triL, negf, negb], axis=1).reshape(128, 512))


def make_in_maps(inputs, ncores=NCORES):
    perm = ffn_in_perm()
    f1 = np.ascontiguousarray(inputs["ffn1_w_in"][:, :, perm])
    f2 = np.ascontiguousarray(inputs["ffn2_w_in"][:, :, perm])
    wmix = np.ascontiguousarray(inputs["mix_w_in"][:, :, _mix_cols()])
    cos_t, sin_t = _rope_tables()
    d = np.arange(128) % 64
    qn, kn = inputs["gqa_q_norm"], inputs["gqa_k_norm"]
    qk_col = np.ascontiguousarray(np.stack([qn[:, d], qn[:, d ^ 16], kn[:, d], kn[:, d ^ 16]], axis=-1)).astype(np.float32)
    cw = inputs["ssd_conv_w"][:, :, 0, :]
    convw = np.ascontiguousarray(cw.reshape(DEPTH, 5, 6, 128).transpose(0, 3, 2, 1))
    convb = np.ascontiguousarray(inputs["ssd_conv_b"].reshape(DEPTH, 6, 128).transpose(0, 2, 1))
    natab = np.stack([_na_table(inputs["na_rpb"][l]) for l in range(DEPTH)], 0).reshape(DEPTH, 3, 128, 8 * 4 * 512)
    shared = {
        "ada_w": inputs["ada_w"], "ada_b": inputs["ada_b"], "ln_g": inputs["ln_g"], "ln_b": inputs["ln_b"],
        "ffn1_w_in": f1, "ffn1_w_out": inputs["ffn1_w_out"], "ffn2_w_in": f2, "ffn2_w_out": inputs["ffn2_w_out"],
        "wmix": wmix, "mix_w_out": inputs["mix_w_out"], "mix_norm_g": inputs["mix_norm_g"],
        "cos_t": cos_t, "sin_t": sin_t, "qk_col": qk_col, "swa_sink": inputs["swa_sink"],
        "convw": convw, "convb": convb,
        "dt_bias": np.ascontiguousarray(inputs["ssd_dt_bias"].reshape(DEPTH, 8, 1)),
        "a_log": np.ascontiguousarray(inputs["ssd_A_log"].reshape(DEPTH, 8, 1)),
        "ssd_D": inputs["ssd_D"], "natab": np.ascontiguousarray(natab), "maska": _swa_mask(), "tri_in": _tri_consts(),
    }
    shared = {k: np.ascontiguousarray(v, dtype=np.float32) for k, v in shared.items()}
    maps = []
    for b in range(ncores):
        m = dict(shared)
        m["x"] = np.ascontiguousarray(inputs["x"][b])
        m["ctx"] = np.ascontiguousarray(inputs["ctx"][b])
        m["c2t"] = np.ascontiguousarray(np.stack([inputs["c"][b], inputs["c_ctx"]], axis=1))
        maps.append(m)
    return maps


def kernel(**inputs):
    inputs = {k: np.asarray(v) for k, v in inputs.items()}
    nc = build_program({})
    maps = make_in_maps(inputs)
    res = run_bass_kernel_spmd(nc, maps, core_ids=list(range(NCORES)))
    return np.stack([r["out"] for r in res.results], axis=0).astype(np.float32)
```
